# Optimizing a Trainium2 kernel written in Bass

```python
import jax, jax.numpy as jnp
from jax import lax
import numpy as np

D_MODEL = 1024
BATCH = 2
SEQ = 8192
DEPTH = 4
DEC_BATCH = 32
DEC_SEQ = 8
PAST_LEN = 8192
PAGE_SIZE = 128

POOL_GROUPS = 4
POOL_GC = D_MODEL // 16
POOL_W = POOL_GROUPS * POOL_GC
POOL_WINDOWS = (2, 4, 8, 16)
POOL_HIST = 15
ATT_HD = 64
DIL_GROUPS = ((128, 1), (512, 4), (2048, 16))
ATT_HPG = 2
ATT_HEADS = ATT_HPG * len(DIL_GROUPS)
ATT_W = ATT_HEADS * ATT_HD
ATT_OUT = ATT_HPG * ATT_HD
BAND_BLK = 128
ROPE_THETA = 10000.0
RWKV_HD = 64
RWKV_HEADS = 6
RWKV_W = RWKV_HEADS * RWKV_HD
DECAY_LORA = 64
AAA_LORA = 64
GATE_LORA = 128
RWKV_PROJ = 3 * RWKV_W + DECAY_LORA + AAA_LORA + GATE_LORA
RWKV_LN_EPS = 64e-5
FFN_HIDDEN = -(-8 * D_MODEL // (3 * 256)) * 256
PLE_DIM = 256
RMS_EPS = 1e-6
IN_SIZES = (POOL_W, ATT_W, ATT_W, ATT_W, RWKV_PROJ, D_MODEL, D_MODEL, D_MODEL)
IN_COLS = POOL_W + 3 * ATT_W + RWKV_PROJ + 3 * D_MODEL
RWKV_SIZES = (RWKV_W, RWKV_W, RWKV_W, DECAY_LORA, AAA_LORA, GATE_LORA)

kernel_name = 'hybrid_pool_dilattn_rwkv7_step'


def split_cols(z, sizes):
    offs, acc = [], 0
    for s in sizes[:-1]:
        acc += s
        offs.append(acc)
    return jnp.split(z, offs, axis=-1)


def rms_norm(x, g):
    xf = x.astype(jnp.float32)
    y = xf * lax.rsqrt(jnp.mean(xf * xf, axis=-1, keepdims=True) + RMS_EPS)
    return (y * g.astype(jnp.float32)).astype(x.dtype)


def rope(x, pos):
    C = x.shape[-1]
    half = C // 2
    inv = ROPE_THETA ** (-2.0 * jnp.arange(half, dtype=jnp.float32) / C)
    ang = pos[:, None] * inv[None, :]
    cos = jnp.cos(ang)[None, :, None, :]
    sin = jnp.sin(ang)[None, :, None, :]
    xf = x.astype(jnp.float32)
    x1, x2 = xf[..., :half], xf[..., half:]
    return jnp.concatenate([x1 * cos - x2 * sin, x2 * cos + x1 * sin], axis=-1).astype(x.dtype)


def pool_mix(u_hist, u_new, pos0, w_grp, scale):
    N, T, _ = u_new.shape
    Hh = u_hist.shape[1]
    ext = jnp.concatenate([u_hist.astype(jnp.float32), u_new.astype(jnp.float32)], axis=1)
    cs = jnp.cumsum(ext, axis=1)
    pos = pos0 + jnp.arange(T)
    means = []
    for g, w in enumerate(POOL_WINDOWS):
        c = cs[..., g * POOL_GC:(g + 1) * POOL_GC]
        lag = jnp.pad(c, ((0, 0), (w, 0), (0, 0)))[:, :c.shape[1]]
        win = (c - lag)[:, Hh:]
        cnt = jnp.minimum(pos + 1, w).astype(jnp.float32)
        means.append(win / cnt[None, :, None])
    d = (jnp.concatenate(means, axis=-1) - ext[:, Hh:]).reshape(N, T, POOL_GROUPS, POOL_GC)
    y = jnp.einsum('ntgc,gcd->ntgd', d, w_grp.astype(jnp.float32)).reshape(N, T, POOL_W)
    return y * scale.astype(jnp.float32), ext[:, -POOL_HIST:]


def band_attn(q, k, v, d, span):
    N, S, H, C = q.shape
    L = S // d
    nb = -(-L // BAND_BLK)
    Lp = nb * BAND_BLK

    def stream(x):
        x = x.astype(jnp.float32).reshape(N, L, d, H, C).transpose(0, 2, 1, 3, 4)
        x = jnp.pad(x, ((0, 0), (0, 0), (0, Lp - L), (0, 0), (0, 0)))
        return x.reshape(N, d, nb, BAND_BLK, H, C)

    def with_prev(x):
        prev = jnp.pad(x, ((0, 0), (0, 0), (1, 0), (0, 0), (0, 0), (0, 0)))[:, :, :nb]
        return jnp.concatenate([prev, x], axis=3)

    qs = stream(q)
    ks = with_prev(stream(k))
    vs = with_prev(stream(v))
    s = jnp.einsum('nrbqhc,nrbkhc->nrbhqk', qs, ks) * (C ** -0.5)
    qi = jnp.arange(BAND_BLK)[:, None]
    kj = jnp.arange(2 * BAND_BLK)[None, :]
    dist = BAND_BLK + qi - kj
    kidx = jnp.arange(nb)[:, None, None] * BAND_BLK + kj[None] - BAND_BLK
    mask = (dist >= 0) & (dist <= span) & (kidx >= 0)
    s = jnp.where(mask[None, None, :, None], s, -jnp.inf)
    m = jnp.max(s, axis=-1, keepdims=True)
    e = jnp.exp(s - m)
    den = jnp.sum(e, axis=-1)
    o = jnp.einsum('nrbhqk,nrbkhc->nrbqhc', e, vs) / den.transpose(0, 1, 2, 4, 3)[..., None]
    lse = (m[..., 0] + jnp.log(den)).transpose(0, 1, 2, 4, 3)
    o = o.reshape(N, d, Lp, H, C)[:, :, :L].transpose(0, 2, 1, 3, 4).reshape(N, S, H, C)
    lse = lse.reshape(N, d, Lp, H)[:, :, :L].transpose(0, 2, 1, 3).reshape(N, S, H)
    return o, lse


def gather_attn(q, k_ext, v_ext, hist_len, d, span):
    N, T, H, C = q.shape
    idx = hist_len + jnp.arange(T)[:, None] - d * jnp.arange(span + 1)[None, :]
    valid = idx >= 0
    idx = jnp.maximum(idx, 0)
    kg = k_ext.astype(jnp.float32)[:, idx]
    vg = v_ext.astype(jnp.float32)[:, idx]
    s = jnp.einsum('nthc,ntkhc->nhtk', q.astype(jnp.float32), kg) * (C ** -0.5)
    s = jnp.where(valid[None, None], s, -jnp.inf)
    m = jnp.max(s, axis=-1, keepdims=True)
    e = jnp.exp(s - m)
    den = jnp.sum(e, axis=-1)
    o = jnp.einsum('nhtk,ntkhc->nthc', e, vg) / den.transpose(0, 2, 1)[..., None]
    lse = (m[..., 0] + jnp.log(den)).transpose(0, 2, 1)
    return o, lse


def merge_dilations(outs, lses):
    o = jnp.stack(outs, axis=0)
    w = jax.nn.softmax(jnp.stack(lses, axis=0), axis=0)
    y = jnp.sum(w[..., None] * o, axis=0)
    return y.reshape(y.shape[0], y.shape[1], ATT_OUT)


def rwkv7_mix(zc, shift_prev, S0, lw):
    N, T, _ = zc.shape
    zc = zc.astype(jnp.float32)
    prev = jnp.concatenate([shift_prev.astype(jnp.float32)[:, None], zc[:, :-1]], axis=1)
    zs = zc + (prev - zc) * lw['rwkv_mu'].astype(jnp.float32)
    r, k, v, zw, za, zg = split_cols(zs, RWKV_SIZES)
    w_log = -jax.nn.softplus(-(lw['rwkv_w0'] + jnp.tanh(zw) @ lw['rwkv_w2'])) - 0.5
    decay = jnp.exp(-jnp.exp(w_log.astype(jnp.float32)))
    a = jax.nn.sigmoid((lw['rwkv_a0'] + za @ lw['rwkv_a2']).astype(jnp.float32))
    g = (jax.nn.sigmoid(zg) @ lw['rwkv_g2']).astype(jnp.float32)
    heads = lambda t: t.astype(jnp.float32).reshape(N, T, RWKV_HEADS, RWKV_HD)
    kk = heads(k * lw['rwkv_k_k'])
    kk = kk * lax.rsqrt(jnp.maximum(jnp.sum(kk * kk, axis=-1, keepdims=True), 1e-24))
    k = k * (1.0 + (a - 1.0) * lw['rwkv_k_a'])
    r, k, v, decay, a = (heads(t) for t in (r, k, v, decay, a))

    def step(S, inp):
        r_t, w_t, k_t, v_t, kk_t, a_t = inp
        sa = jnp.einsum('nhvk,nhk->nhv', S, -kk_t)
        S = (S * w_t[:, :, None, :] + sa[..., None] * (kk_t * a_t)[:, :, None, :]
             + v_t[..., None] * k_t[:, :, None, :])
        return S, jnp.einsum('nhvk,nhk->nhv', S, r_t)

    seq = tuple(t.transpose(1, 0, 2, 3) for t in (r, decay, k, v, kk, a))
    S_T, y = lax.scan(step, S0.astype(jnp.float32), seq)
    y = y.transpose(1, 0, 2, 3)
    mu = jnp.mean(y, axis=-1, keepdims=True)
    var = jnp.mean(jnp.square(y - mu), axis=-1, keepdims=True)
    yn = ((y - mu) * lax.rsqrt(var + RWKV_LN_EPS)).reshape(N, T, RWKV_W)
    yn = yn * lw['rwkv_ln_g'].astype(jnp.float32) + lw['rwkv_ln_b'].astype(jnp.float32)
    bonus = (jnp.sum(r * k * lw['rwkv_r_k'].astype(jnp.float32), axis=-1, keepdims=True) * v).reshape(N, T, RWKV_W)
    return (yn + bonus) * g, zc[:, -1], S_T


def decoder_layer(x, p_l, pos0, pool_hist, shift_prev, wkv0, kv_bufs, lw, prompt):
    N, T, _ = x.shape
    h = rms_norm(x, lw['norm_mix_g'])
    z = h @ lw['w_in']
    z_pool, q, k, v, z_rwkv, g_pool, g_attn, g_rwkv = split_cols(z, IN_SIZES)
    y_pool, new_pool = pool_mix(pool_hist, z_pool, pos0, lw['pool_w_grp'], lw['pool_scale'])
    pos = (pos0 + jnp.arange(T)).astype(jnp.float32)
    q = rope(q.reshape(N, T, ATT_HEADS, ATT_HD), pos)
    k = rope(k.reshape(N, T, ATT_HEADS, ATT_HD), pos)
    v = v.reshape(N, T, ATT_HEADS, ATT_HD)
    outs, lses, new_kv = [], [], []
    for gi, (win, dil) in enumerate(DIL_GROUPS):
        hs = slice(gi * ATT_HPG, (gi + 1) * ATT_HPG)
        qg, kg, vg = q[:, :, hs], k[:, :, hs], v[:, :, hs]
        if prompt:
            o, lse = band_attn(qg, kg, vg, dil, win // dil)
            keep = min(win, T)
            new_kv.append(jnp.stack([kg[:, T - keep:], vg[:, T - keep:]], axis=2))
        else:
            buf = kv_bufs[gi]
            hist = buf.shape[1]
            k_ext = jnp.concatenate([buf[:, :, 0].astype(kg.dtype), kg], axis=1)
            v_ext = jnp.concatenate([buf[:, :, 1].astype(vg.dtype), vg], axis=1)
            o, lse = gather_attn(qg, k_ext, v_ext, hist, dil, win // dil)
            keep = min(win, hist + T)
            new_kv.append(jnp.stack([k_ext[:, -keep:], v_ext[:, -keep:]], axis=2))
        outs.append(o)
        lses.append(lse)
    y_attn = merge_dilations(outs, lses)
    y_rwkv, new_shift, new_wkv = rwkv7_mix(z_rwkv, shift_prev, wkv0, lw)
    merged = (jax.nn.sigmoid(g_pool) * (y_pool @ lw['proj_pool'])
              + jax.nn.sigmoid(g_attn) * (y_attn @ lw['proj_attn'])
              + jax.nn.sigmoid(g_rwkv) * (y_rwkv @ lw['proj_rwkv']))
    x = x + (merged @ lw['w_out']).astype(x.dtype)
    h = rms_norm(x, lw['norm_ffn_g'])
    x = x + ((jax.nn.silu(h @ lw['ffn_w1']) * (h @ lw['ffn_w3'])) @ lw['ffn_w2']).astype(x.dtype)
    gate = jax.nn.sigmoid(rms_norm(x, lw['norm_ple_g']) @ lw['ple_gate'])
    x = x + ((p_l @ lw['ple_proj']) * gate).astype(x.dtype)
    return x, (new_pool, new_shift, new_wkv, new_kv)


def setup_inputs(seed: int = 0) -> dict:
    key = jax.random.key(seed)
    ks = iter(jax.random.split(key, 48))
    f32 = jnp.float32

    def nrm(shape, scale=1.0):
        return jax.random.normal(next(ks), shape, f32) * scale

    def gain(shape):
        return 1.0 + 0.02 * jax.random.normal(next(ks), shape, f32)

    L = DEPTH
    kv_len = [min(w, PAST_LEN) for w, _ in DIL_GROUPS]
    return {
        'x_prompt': nrm((BATCH, SEQ, D_MODEL)),
        'x_sample': nrm((DEC_BATCH, DEC_SEQ, D_MODEL)),
        'state_pool': nrm((L, DEC_BATCH, POOL_HIST, POOL_W)),
        'state_shift': nrm((L, DEC_BATCH, RWKV_PROJ)),
        'state_wkv': nrm((L, DEC_BATCH, RWKV_HEADS, RWKV_HD, RWKV_HD), 0.3),
        'cache_kv_w128': nrm((L, DEC_BATCH, kv_len[0], 2, ATT_HPG, ATT_HD)),
        'cache_kv_w512': nrm((L, DEC_BATCH, kv_len[1], 2, ATT_HPG, ATT_HD)),
        'cache_kv_w2048': nrm((L, DEC_BATCH, kv_len[2], 2, ATT_HPG, ATT_HD)),
        'p_prompt': nrm((L, BATCH, SEQ, PLE_DIM)),
        'p_sample': nrm((L, DEC_BATCH, DEC_SEQ, PLE_DIM)),
        'norm_mix_g': gain((L, D_MODEL)),
        'w_in': nrm((L, D_MODEL, IN_COLS), D_MODEL ** -0.5),
        'pool_w_grp': nrm((L, POOL_GROUPS, POOL_GC, POOL_GC), POOL_GC ** -0.5),
        'pool_scale': 1.0 + nrm((L, POOL_W), 0.1),
        'rwkv_mu': jax.random.uniform(next(ks), (L, RWKV_PROJ), f32),
        'rwkv_w0': nrm((L, RWKV_W), 0.5),
        'rwkv_w2': nrm((L, DECAY_LORA, RWKV_W), 0.5 * DECAY_LORA ** -0.5),
        'rwkv_a0': nrm((L, RWKV_W), 0.1),
        'rwkv_a2': nrm((L, AAA_LORA, RWKV_W), AAA_LORA ** -0.5),
        'rwkv_g2': nrm((L, GATE_LORA, RWKV_W), GATE_LORA ** -0.5),
        'rwkv_k_k': 1.0 + nrm((L, RWKV_W), 0.1),
        'rwkv_k_a': 1.0 + nrm((L, RWKV_W), 0.1),
        'rwkv_r_k': nrm((L, RWKV_HEADS, RWKV_HD), 0.1),
        'rwkv_ln_g': gain((L, RWKV_W)),
        'rwkv_ln_b': nrm((L, RWKV_W), 0.02),
        'proj_pool': nrm((L, POOL_W, D_MODEL), POOL_W ** -0.5),
        'proj_attn': nrm((L, ATT_OUT, D_MODEL), ATT_OUT ** -0.5),
        'proj_rwkv': nrm((L, RWKV_W, D_MODEL), RWKV_W ** -0.5),
        'w_out': nrm((L, D_MODEL, D_MODEL), D_MODEL ** -0.5),
        'norm_ffn_g': gain((L, D_MODEL)),
        'ffn_w1': nrm((L, D_MODEL, FFN_HIDDEN), D_MODEL ** -0.5),
        'ffn_w3': nrm((L, D_MODEL, FFN_HIDDEN), D_MODEL ** -0.5),
        'ffn_w2': nrm((L, FFN_HIDDEN, D_MODEL), FFN_HIDDEN ** -0.5),
        'norm_ple_g': gain((L, D_MODEL)),
        'ple_proj': nrm((L, PLE_DIM, D_MODEL), PLE_DIM ** -0.5),
        'ple_gate': nrm((L, D_MODEL, D_MODEL), D_MODEL ** -0.5),
        'norm_final_g': gain((D_MODEL,)),
    }


def reference(x_prompt, x_sample, state_pool, state_shift, state_wkv, cache_kv_w128, cache_kv_w512,
              cache_kv_w2048, p_prompt, p_sample, norm_mix_g, w_in, pool_w_grp, pool_scale, rwkv_mu,
              rwkv_w0, rwkv_w2, rwkv_a0, rwkv_a2, rwkv_g2, rwkv_k_k, rwkv_k_a, rwkv_r_k, rwkv_ln_g,
              rwkv_ln_b, proj_pool, proj_attn, proj_rwkv, w_out, norm_ffn_g, ffn_w1, ffn_w3, ffn_w2,
              norm_ple_g, ple_proj, ple_gate, norm_final_g):
    xp, xs = x_prompt, x_sample
    Np = xp.shape[0]
    pool_p, pool_s, shift_p, shift_s, wkv_p, wkv_s, kv_p, kv_s = [], [], [], [], [], [], [], []
    for i in range(DEPTH):
        lw = {
            'norm_mix_g': norm_mix_g[i], 'w_in': w_in[i], 'pool_w_grp': pool_w_grp[i],
            'pool_scale': pool_scale[i], 'rwkv_mu': rwkv_mu[i], 'rwkv_w0': rwkv_w0[i],
            'rwkv_w2': rwkv_w2[i], 'rwkv_a0': rwkv_a0[i], 'rwkv_a2': rwkv_a2[i], 'rwkv_g2': rwkv_g2[i],
            'rwkv_k_k': rwkv_k_k[i], 'rwkv_k_a': rwkv_k_a[i], 'rwkv_r_k': rwkv_r_k[i],
            'rwkv_ln_g': rwkv_ln_g[i], 'rwkv_ln_b': rwkv_ln_b[i], 'proj_pool': proj_pool[i],
            'proj_attn': proj_attn[i], 'proj_rwkv': proj_rwkv[i], 'w_out': w_out[i],
            'norm_ffn_g': norm_ffn_g[i], 'ffn_w1': ffn_w1[i], 'ffn_w3': ffn_w3[i], 'ffn_w2': ffn_w2[i],
            'norm_ple_g': norm_ple_g[i], 'ple_proj': ple_proj[i], 'ple_gate': ple_gate[i],
        }
        xp, (a_pool, a_shift, a_wkv, a_kv) = decoder_layer(
            xp, p_prompt[i], 0,
            jnp.zeros((Np, 0, POOL_W), xp.dtype),
            jnp.zeros((Np, RWKV_PROJ), jnp.float32),
            jnp.zeros((Np, RWKV_HEADS, RWKV_HD, RWKV_HD), jnp.float32),
            None, lw, True)
        xs, (b_pool, b_shift, b_wkv, b_kv) = decoder_layer(
            xs, p_sample[i], PAST_LEN, state_pool[i], state_shift[i], state_wkv[i],
            (cache_kv_w128[i], cache_kv_w512[i], cache_kv_w2048[i]), lw, False)
        pool_p.append(a_pool); shift_p.append(a_shift); wkv_p.append(a_wkv); kv_p.append(a_kv)
        pool_s.append(b_pool); shift_s.append(b_shift); wkv_s.append(b_wkv); kv_s.append(b_kv)
    y_prompt = rms_norm(xp, norm_final_g)
    y_sample = rms_norm(xs, norm_final_g)
    new_pool_p = jnp.stack(pool_p)
    new_pool_s = jnp.stack(pool_s)
    new_shift_p = jnp.stack(shift_p)
    new_shift_s = jnp.stack(shift_s)
    new_wkv_p = jnp.stack(wkv_p)
    new_wkv_s = jnp.stack(wkv_s)
    kv128_p = jnp.stack([kv[0] for kv in kv_p])
    kv128_s = jnp.stack([kv[0] for kv in kv_s])
    kv512_p = jnp.stack([kv[1] for kv in kv_p])
    kv512_s = jnp.stack([kv[1] for kv in kv_s])
    kv2048_p = jnp.stack([kv[2] for kv in kv_p])
    kv2048_s = jnp.stack([kv[2] for kv in kv_s])
    return (y_prompt, y_sample, new_pool_p, new_pool_s, new_shift_p, new_shift_s, new_wkv_p, new_wkv_s,
            kv128_p, kv128_s, kv512_p, kv512_s, kv2048_p, kv2048_s)
```

```python
import math
from contextlib import ExitStack
import numpy as np
import concourse.bass as bass
import concourse.mybir as mybir
from concourse.bass_utils import run_bass_kernel_spmd

F32 = mybir.dt.float32
BF = mybir.dt.bfloat16
AF = mybir.ActivationFunctionType
ALU = mybir.AluOpType
AX = mybir.AxisListType

D = 1024
KC = 8
NCOL = 5888
FF = 2816
FC = 22
PLE = 256
NSQ = 4
DS = 8
TS = NSQ * DS
RW = 1408
GROUPS = ((128, 1), (512, 4), (2048, 16))
C0 = math.exp(-0.5)
ENGS = ['pe', 'act', 'dve', 'pool', 'sp']
BLK = {'pe': 'tensor', 'act': 'scalar', 'dve': 'vector', 'pool': 'gpsimd', 'sp': 'sync'}


class Sched:
    def __init__(self, nc, stack):
        self.nc = nc
        self.stack = stack
        self.ops = []
        self.cnt = {}
        self.dcnt = {}
        self.W = {}
        self.R = {}
        self.waited = {e: {} for e in ENGS}
        self.sems = {}

    def sem(self, lane, idx):
        dma = lane.startswith('dma')
        eps = 1900 if dma else 16000
        ep = (idx - 1) // eps
        key = (lane, ep)
        if key not in self.sems:
            self.sems[key] = self.stack.enter_context(self.nc.semaphore(f"s_{lane}_{ep}"))
        return self.sems[key], ((idx - 1) % eps + 1) * (16 if dma else 1)

    NSLOT = 8
    limit = None
    nflush = 0

    def add(self, eng, fn, reads, writes, dma=False):
        if self.limit is not None and self.nflush >= self.limit:
            return
        deps = {}
        if dma:
            k = self.dcnt[eng] = self.dcnt.get(eng, 0) + 1
            lane = f"dma_{eng}_{(k - 1) % self.NSLOT}"
            idx = self.cnt[lane] = self.cnt.get(lane, 0) + 1
            if idx > 1:
                deps[lane] = idx - 1
        else:
            lane = eng
            idx = self.cnt[lane] = self.cnt.get(lane, 0) + 1
        for k in reads:
            for l, i in self.W.get(k, {}).items():
                if l == lane and eng == 'pe' and not dma:
                    continue
                deps[l] = max(deps.get(l, 0), i)
        for k in writes:
            for l, i in self.W.get(k, {}).items():
                if l == lane and not dma:
                    continue
                deps[l] = max(deps.get(l, 0), i)
            for l, i in self.R.get(k, {}).items():
                if l == lane and not dma:
                    continue
                deps[l] = max(deps.get(l, 0), i)
        waits = []
        wd = self.waited[eng]
        for l, i in deps.items():
            if wd.get(l, 0) >= i:
                continue
            wd[l] = i
            waits.append(self.sem(l, i))
        for k in reads:
            self.R.setdefault(k, {})[lane] = idx
        for k in writes:
            self.W[k] = {lane: idx}
            self.R[k] = {}
        self.ops.append((eng, fn, waits, self.sem(lane, idx), 16 if dma else 1))

    def barrier(self):
        for eng in ENGS:
            waits = []
            wd = self.waited[eng]
            for l, i in self.cnt.items():
                if wd.get(l, 0) >= i:
                    continue
                wd[l] = i
                waits.append(self.sem(l, i))
            if waits:
                self.ops.append((eng, None, waits, None, 0))
        self.W = {}
        self.R = {}

    def maybe_flush(self, max_ops=2500, max_dma=40):
        nd = {}
        for o in self.ops:
            if o[4] == 16:
                nd[o[0]] = nd.get(o[0], 0) + 1
        if len(self.ops) > max_ops or (nd and max(nd.values()) > max_dma):
            self.nflush -= 1
            self.flush()

    def flush(self):
        self.nflush += 1
        self.barrier()
        ops = self.ops
        self.ops = []
        if not ops:
            return
        with self.nc.Block() as block:
            for eng in ENGS:
                mine = [o for o in ops if o[0] == eng]
                if not mine:
                    continue

                def body(e, mine=mine):
                    for (_, fn, waits, sv, inc) in mine:
                        if fn is None:
                            for (ws, wv) in waits:
                                e.wait_ge(ws, wv)
                            continue
                        for (ws, wv) in waits[:-1]:
                            e.wait_ge(ws, wv)
                        ins = fn(e)
                        if waits:
                            ins._wait_ge(waits[-1][0], waits[-1][1])
                        ins.then_inc(sv[0], inc)
                getattr(block, BLK[eng])(body)

    @staticmethod
    def nm(*aps):
        return [a.tensor.name for a in aps if a is not None and hasattr(a, 'tensor')]

    def mm(self, out, lhsT, rhs, start=True, stop=True):
        self.add('pe', lambda e: e.matmul(out, lhsT, rhs, start=start, stop=stop), self.nm(lhsT, rhs), self.nm(out))

    def tr(self, out, in_, ident):
        self.add('pe', lambda e: e.transpose(out, in_, ident), self.nm(in_, ident), self.nm(out))

    def act(self, out, in_, func, bias=None, scale=1.0):
        kw = {}
        if bias is not None:
            kw['bias'] = bias
        self.add('act', lambda e: e.activation(out, in_, func, scale=scale, **kw), self.nm(in_, bias, scale), self.nm(out))

    def cp(self, eng, out, in_):
        if eng == 'act':
            self.act(out, in_, AF.Copy)
        else:
            self.add(eng, lambda e: e.tensor_copy(out, in_), self.nm(in_), self.nm(out))

    def tt(self, eng, out, in0, in1, op):
        self.add(eng, lambda e: e.tensor_tensor(out, in0, in1, op), self.nm(in0, in1), self.nm(out))

    def ts(self, eng, out, in0, s1, s2, op0, op1=None):
        if op1 is None:
            self.add(eng, lambda e: e.tensor_scalar(out, in0, s1, None, op0), self.nm(in0, s1), self.nm(out))
        else:
            self.add(eng, lambda e: e.tensor_scalar(out, in0, s1, s2, op0, op1), self.nm(in0, s1, s2), self.nm(out))

    def stt(self, eng, out, in0, sc, in1, op0, op1):
        eng = 'dve'
        self.add(eng, lambda e: e.scalar_tensor_tensor(out, in0, sc, in1, op0, op1), self.nm(in0, sc, in1), self.nm(out))

    def ms(self, eng, ap, val):
        self.add(eng, lambda e: e.memset(ap, val), [], self.nm(ap))

    def red(self, eng, out, in_, op):
        self.add(eng, lambda e: e.tensor_reduce(out, in_, AX.X, op), self.nm(in_), self.nm(out))

    def scan(self, out, d0, d1, init, op0, op1):
        self.add('dve', lambda e: e.tensor_tensor_scan(out, d0, d1, init, op0, op1), self.nm(d0, d1), self.nm(out))

    def dma(self, eng, out, in_, slow=False):
        if slow:
            fn = lambda e: e.dma_start(out=out, in_=in_, allow_slow_non_contiguous=True)
        else:
            fn = lambda e: e.dma_start(out=out, in_=in_)
        self.add(eng, fn, self.nm(in_), self.nm(out), dma=True)


class Rot:
    uid = 0

    def __init__(self, nc, stack, name, shape, dtype, n, psum=False):
        mk = nc.psum_tensor if psum else nc.sbuf_tensor
        Rot.uid += 1
        self.t = [stack.enter_context(mk(f"{name}{i}_u{Rot.uid}", shape, dtype)) for i in range(n)]
        self.i = 0

    def nxt(self):
        t = self.t[self.i % len(self.t)]
        self.i += 1
        return t


def build(T, L, debug=False, first=True, last=True):
    TT = T + TS
    NTP = T // 512
    nc = bass.Bass("TRN2", target_bir_lowering=False)
    dt_in = lambda n, s, d=F32: nc.dram_tensor(n, list(s), d, kind="ExternalInput").ap()
    dt_out = lambda n, s, d=F32: nc.dram_tensor(n, list(s), d, kind="ExternalOutput").ap()
    dt_scr = lambda n, s, d=F32: nc.dram_tensor(n, list(s), d, kind="ExternalOutput" if debug else "Internal").ap()

    if first:
        x_in = dt_in("x_in", [TT, D])
    else:
        xT_in = dt_in("xT_in", [128, KC, TT])
    p_in = dt_in("p_in", [L, TT, PLE])
    st_pool = dt_in("st_pool", [L, NSQ, 15, 256])
    st_shift = dt_in("st_shift", [L, NSQ, RW])
    st_wkv = dt_in("st_wkv", [L, NSQ, 6, 64, 64])
    cache = [dt_in(f"cache{w}", [L, NSQ, w, 256]) for w, _ in GROUPS]
    w_in = dt_in("w_in", [L, D, NCOL])
    pool_wblk = dt_in("pool_wblk", [L, 2, 128, 128])
    vec8 = dt_in("vec8", [L, 128, 3 * KC])
    vecf = dt_in("vecf", [128, KC])
    pool_scale = dt_in("pool_scale", [L, 128, 2])
    rw_mu = dt_in("rw_mu", [L, 128, 11])
    rw_vec3 = dt_in("rw_vec3", [L, 128, 5 * 3])
    rw_w2a2 = dt_in("rw_w2a2", [L, 128, 384])
    rw_g2 = dt_in("rw_g2", [L, 128, 384])
    rw_ln = dt_in("rw_ln", [L, 2, 384])
    proj_pool = dt_in("proj_pool", [L, 256, D])
    proj_attn = dt_in("proj_attn", [L, 128, D])
    proj_rwkv = dt_in("proj_rwkv", [L, 384, D])
    w_out = dt_in("w_out", [L, D, D])
    ffn_w1 = dt_in("ffn_w1", [L, D, FF])
    ffn_w3 = dt_in("ffn_w3", [L, D, FF])
    ffn_w2 = dt_in("ffn_w2", [L, FF, D])
    ple_proj = dt_in("ple_proj", [L, PLE, D])
    ple_gate = dt_in("ple_gate", [L, D, D])
    c_cos = dt_in("c_cos", [TT, 384])
    c_sin = dt_in("c_sin", [TT, 384])
    c_ident = dt_in("c_ident", [128, 128])
    c_blk = dt_in("c_blk", [128, 128])
    c_masks = dt_in("c_masks", [128, 4, 128])
    c_invcnt = dt_in("c_invcnt", [128, 2, 2, 512])
    c_smask = dt_in("c_smask", [128, 21, 8])
    c_smask_new = dt_in("c_smask_new", [8, 3, 8])
    c_reset = dt_in("c_reset", [128, 2, 512])

    y_out = dt_out("y_out", [TT, D]) if last else None
    pool_o = dt_out("pool_o", [L, 1 + NSQ, 15, 256])
    shift_o = dt_out("shift_o", [L, 1 + NSQ, RW])
    wkv_o = dt_out("wkv_o", [L, 1 + NSQ, 6, 64, 64])
    kv_o = [dt_out(f"kv{w}_o", [L, 1 + NSQ, w, 256]) for w, _ in GROUPS]

    xT = dt_scr("xT", [128, KC, TT]) if last else dt_out("xT", [128, KC, TT])
    zpT = dt_scr("zpT", [128, 2, TT])
    zrT = dt_scr("zrT", [128, 11, TT])
    gT = dt_scr("gT", [128, 24, TT], BF)
    qk_tok = dt_scr("qk_tok", [TT, 768], BF)
    v_tok = dt_scr("v_tok", [TT, 6, 65], BF)
    og = [dt_scr(f"og{g}", [T, 2, 65]) for g in range(3)]
    ypT = dt_scr("ypT", [128, 2, TT], BF)
    yaT = dt_scr("yaT", [128, TT], BF)
    yrT = dt_scr("yrT", [128, 3, TT], BF)

    top = ExitStack()
    S = Sched(nc, top)
    with top:
        ident = top.enter_context(nc.sbuf_tensor("ident", [128, 128], F32))
        identb = top.enter_context(nc.sbuf_tensor("identb", [128, 128], BF))
        blkones = top.enter_context(nc.sbuf_tensor("blkones", [128, 128], BF))
        onesb = top.enter_context(nc.sbuf_tensor("onesb", [128, 128], BF))
        masks = top.enter_context(nc.sbuf_tensor("masks", [128, 4, 128], F32))
        masksb = top.enter_context(nc.sbuf_tensor("masksb", [128, 4, 128], BF))
        blkf = top.enter_context(nc.sbuf_tensor("blkf", [128, 128], F32))
        S.dma('sp', ident[:], c_ident)
        S.dma('sp', blkf[:], c_blk)
        S.dma('sp', masks[:], c_masks)
        S.cp('dve', identb[:], ident[:])
        S.cp('dve', blkones[:], blkf[:])
        S.cp('dve', masksb[:], masks[:])
        S.ms('dve', onesb[:], 1.0)

        PS = Rot(nc, top, "ps", [128, 512], F32, 5, psum=True)
        PA = top.enter_context(nc.psum_tensor("psacc", [128, 512], F32))
        PSB = Rot(nc, top, "psb", [128, 1024], BF, 2, psum=True)

        with ExitStack() as ph:
            xin = Rot(nc, ph, "p0x", [128, D], F32, 2)
            xst = Rot(nc, ph, "p0s", [128, KC, 128], F32, 2)
            blocks = [(i * 128, 128) for i in range(T // 128)] + [(T, TS)]
            if not first:
                blocks = []
                for kc in range(KC):
                    S.dma('sp' if kc % 2 == 0 else 'act', xT[:, kc, :], xT_in[:, kc, :])
            for (t0, n) in blocks:
                xi = xin.nxt()
                S.dma('sp', xi[0:n, :], x_in[t0:t0 + n, :])
                st = xst.nxt()
                for half in range(2):
                    ps = PS.nxt()
                    for q in range(4):
                        kc = half * 4 + q
                        S.tr(ps[:, q * 128:q * 128 + n], xi[0:n, kc * 128:(kc + 1) * 128], ident[0:n, 0:n])
                    src = ps[:].rearrange("p (q t) -> p q t", q=4)[:, :, 0:n]
                    S.cp('act' if half == 0 else 'dve', st[:, half * 4:half * 4 + 4, 0:n], src)
                S.dma('pool', xT[:, :, t0:t0 + n], st[:, :, 0:n])
                S.maybe_flush()
            for gi, (w, dil) in enumerate(GROUPS):
                for l in range(L):
                    S.dma('sp', kv_o[gi][l, 1:1 + NSQ, 0:w - DS, :], cache[gi][l, :, DS:w, :])
            for l in range(L):
                S.dma('sp', pool_o[l, 1:1 + NSQ, 0:7, :], st_pool[l, :, 8:15, :])
            S.flush()

        for l in range(L):
            layer(nc, S, locals(), l)
        if last:
            final_out(nc, S, locals())
    return nc


def rmsnorm_tile(S, PS, x, h, g, sqp, rsp, onesb, n):
    sq = sqp.nxt()
    S.act(sq[:, :, 0:n], x[:, :, 0:n], AF.Square)
    ps = PS.nxt()
    for kc in range(KC):
        S.mm(ps[:, 0:n], onesb[:], sq[:, kc, 0:n], start=(kc == 0), stop=(kc == KC - 1))
    rs = rsp.nxt()
    S.ts('dve', rs[:, 0:n], ps[:, 0:n], 1.0 / D, 1e-6, ALU.mult, ALU.add)
    S.act(rs[:, 0:n], rs[:, 0:n], AF.Ln)
    S.act(rs[:, 0:n], rs[:, 0:n], AF.Exp, scale=-0.5)
    for kc in range(KC):
        S.stt('dve' if kc % 2 == 0 else 'pool', h[:, kc, 0:n], x[:, kc, 0:n], g[:, kc:kc + 1], rs[:, 0:n], ALU.mult, ALU.mult)
    return rs


def load_cast(S, stage, dst, src, n_rows_chunks, engs=('dve', 'pool', 'act')):
    for c in range(n_rows_chunks):
        st = stage.nxt()
        n = src.shape[1]
        S.dma('sp' if c % 2 == 0 else 'act', st[:, 0:n], src[c * 128:(c + 1) * 128, :])
        S.cp(engs[c % len(engs)], dst[:, c, :], st[:, 0:n])


def layer(nc, S, E, l):
    g = E.get
    T, TT, L = E['T'], E['TT'], E['L']
    NTP = E['NTP']
    PS, PSB = E['PS'], E['PSB']
    ident, identb, blkones, onesb, masks, masksb = E['ident'], E['identb'], E['blkones'], E['onesb'], E['masks'], E['masksb']
    xT, zpT, zrT, gT, qk_tok, v_tok, og, ypT, yaT, yrT = (E[k] for k in ('xT', 'zpT', 'zrT', 'gT', 'qk_tok', 'v_tok', 'og', 'ypT', 'yaT', 'yrT'))
    kv_o, pool_o, shift_o, wkv_o, y_out = E['kv_o'], E['pool_o'], E['shift_o'], E['wkv_o'], E['y_out']
    tiles = [(i * 512, 512) for i in range(NTP)] + [(T, TS)]
    tiles1 = [(i * 256, 256) for i in range(T // 256)] + [(T, TS)]
    xTv = xT

    with ExitStack() as ph:
        sb = lambda n, s, d=F32: ph.enter_context(nc.sbuf_tensor(f"{n}_L{l}", s, d))
        wbf = sb("p1w", [128, KC, NCOL], BF)
        stage = Rot(nc, ph, "p1st", [128, NCOL // 2], F32, 2)
        for kc in range(KC):
            for hf in range(2):
                st = stage.nxt()
                c0 = hf * (NCOL // 2)
                S.dma('sp' if hf == 0 else 'act', st[:], E['w_in'][l, kc * 128:(kc + 1) * 128, c0:c0 + NCOL // 2])
                S.cp(('dve', 'pool')[(kc * 2 + hf) % 2], wbf[:, kc, c0:c0 + NCOL // 2], st[:])
        gv = sb("p1g", [128, 3 * KC])
        S.dma('sp', gv[:], E['vec8'][l])
        xp = Rot(nc, ph, "p1x", [128, KC, 256], F32, 2)
        hp = Rot(nc, ph, "p1h", [128, KC, 256], BF, 2)
        sqp = Rot(nc, ph, "p1sq", [128, KC, 256], BF, 1)
        rsp = Rot(nc, ph, "p1rs", [128, 256], F32, 2)
        stf = Rot(nc, ph, "p1stf", [128, 6, 256], F32, 2)
        stg = Rot(nc, ph, "p1stg", [128, 8, 256], BF, 2)
        cosp = Rot(nc, ph, "p1cos", [128, 2, 384], F32, 2)
        sinp = Rot(nc, ph, "p1sin", [128, 2, 384], F32, 2)
        tmp = Rot(nc, ph, "p1tmp", [128, 6, 32], F32, 4)
        qbf = Rot(nc, ph, "p1qb", [128, 768], BF, 2)
        kf = Rot(nc, ph, "p1kf", [128, 384], F32, 2)
        vf = Rot(nc, ph, "p1vf", [128, 384], F32, 2)
        vb = [sb(f"p1vb{i}", [128, 6, 65], BF) for i in range(2)]
        for t_ in vb:
            S.ms('pool', t_[:], 1.0)
        zpv = zpT
        zrv = zrT
        gTv = gT
        vbi = 0
        for (t0, n) in tiles1:
            S.maybe_flush()
            sample = (t0 == T)
            x = xp.nxt()
            S.dma('sp', x[:, :, 0:n], xTv[:, :, t0:t0 + n])
            cs, sn = cosp.nxt(), sinp.nxt()
            nb = (n + 127) // 128
            pb = min(n, 128)
            S.dma('act', cs[0:pb, 0:nb, :], E['c_cos'][t0:t0 + n, :].rearrange("(b p) c -> p b c", p=pb))
            S.dma('act', sn[0:pb, 0:nb, :], E['c_sin'][t0:t0 + n, :].rearrange("(b p) c -> p b c", p=pb))
            h = hp.nxt()
            rmsnorm_tile(S, PS, x, h, gv[:, 0:KC], sqp, rsp, onesb, n)

            def proj(c):
                ps = PS.nxt()
                for kc in range(KC):
                    S.mm(ps[:, 0:n], wbf[:, kc, c * 128:(c + 1) * 128], h[:, kc, 0:n], start=(kc == 0), stop=(kc == KC - 1))
                return ps
            st = stf.nxt()
            for c in range(2):
                ps = proj(c)
                S.cp('act', st[:, c, 0:n], ps[:, 0:n])
            S.dma('pool', zpv[:, :, t0:t0 + n], st[:, 0:2, 0:n])
            if sample:
                for s in range(NSQ):
                    for c in range(2):
                        S.dma('pool', pool_o[l, 1 + s, 7:15, c * 128:(c + 1) * 128].rearrange("t p -> p t"), st[:, c, s * DS:(s + 1) * DS], slow=True)
            elif t0 + n == T:
                for c in range(2):
                    S.dma('pool', pool_o[l, 0, :, c * 128:(c + 1) * 128].rearrange("t p -> p t"), st[:, c, n - 15:n], slow=True)
            for part, cl in enumerate(((0, 6), (6, 11))):
                st = stf.nxt()
                for j in range(cl[0], cl[1]):
                    ps = proj(11 + j)
                    S.cp('act' if j % 2 == 0 else 'dve', st[:, j - cl[0], 0:n], ps[:, 0:n])
                S.dma('pool', zrv[:, cl[0]:cl[1], t0:t0 + n], st[:, 0:cl[1] - cl[0], 0:n])
                if sample:
                    for s in range(NSQ):
                        S.dma('pool', shift_o[l, 1 + s, cl[0] * 128:cl[1] * 128].rearrange("(c p) -> p c", p=128),
                              st[:, 0:cl[1] - cl[0], s * DS + DS - 1], slow=True)
                elif t0 + n == T:
                    S.dma('pool', shift_o[l, 0, cl[0] * 128:cl[1] * 128].rearrange("(c p) -> p c", p=128),
                          st[:, 0:cl[1] - cl[0], n - 1], slow=True)
            for gi in range(3):
                sg_ = stg.nxt()
                for j in range(8):
                    ps = proj(22 + gi * 8 + j)
                    S.act(sg_[:, j, 0:n], ps[:, 0:n], AF.Sigmoid)
                S.dma('pool', gTv[:, gi * 8:(gi + 1) * 8, t0:t0 + n], sg_[:, :, 0:n])
            for tb in range(nb):
                m = min(128, n - tb * 128)
                tok0 = t0 + tb * 128
                pss = []
                for ci in range(3):
                    ps = PS.nxt()
                    c0 = 256 + ci * 384
                    for kc in range(KC):
                        S.mm(ps[0:m, 0:384], h[:, kc, tb * 128:tb * 128 + m], wbf[:, kc, c0:c0 + 384], start=(kc == 0), stop=(kc == KC - 1))
                    pss.append(ps)
                qb = qbf.nxt()
                kf_ = kf.nxt()
                for qi in range(2):
                    src = pss[qi][0:m, 0:384].rearrange("p (h two c) -> p h two c", two=2, c=32)
                    x1, x2 = src[:, :, 0, :], src[:, :, 1, :]
                    cc = cs[0:m, tb, :].rearrange("p (h c) -> p h c", c=32)[:, 0:6, :] if False else cs[0:m, tb, 0:192].rearrange("p (h c) -> p h c", c=32)
                    ss_ = sn[0:m, tb, 0:192].rearrange("p (h c) -> p h c", c=32)
                    if qi == 0:
                        dst = qb[0:m, 0:384].rearrange("p (h two c) -> p h two c", two=2, c=32)
                    else:
                        dst = kf_[0:m, :].rearrange("p (h two c) -> p h two c", two=2, c=32)
                    e1, e2 = ('dve', 'dve')
                    t1, t2, t3, t4 = tmp.nxt(), tmp.nxt(), tmp.nxt(), tmp.nxt()
                    S.tt('dve', t1[0:m], x1, cc, ALU.mult)
                    S.tt('dve', t2[0:m], x2, ss_, ALU.mult)
                    S.tt('pool', dst[:, :, 0, :], t1[0:m], t2[0:m], ALU.subtract)
                    S.tt('dve', t3[0:m], x2, cc, ALU.mult)
                    S.tt('dve', t4[0:m], x1, ss_, ALU.mult)
                    S.tt('pool', dst[:, :, 1, :], t3[0:m], t4[0:m], ALU.add)
                S.cp('pool', qb[0:m, 384:768], kf_[0:m, :])
                S.dma('pool', qk_tok[tok0:tok0 + m, :], qb[0:m, :])
                vf_ = vf.nxt()
                S.cp('act', vf_[0:m, :], pss[2][0:m, 0:384])
                vb_ = vb[vbi % 2]
                vbi += 1
                S.cp('pool', vb_[0:m, :, 0:64], vf_[0:m, :].rearrange("p (h c) -> p h c", c=64))
                S.dma('pool', v_tok[tok0:tok0 + m, :, :], vb_[0:m, :, :])
                for gi, (w, dil) in enumerate(GROUPS):
                    kk_ = kf_[:, gi * 128:(gi + 1) * 128]
                    vv_ = vf_[:, gi * 128:(gi + 1) * 128]
                    if sample:
                        for s in range(NSQ):
                            dst = kv_o[gi][l, 1 + s, w - DS:w, :]
                            S.dma('pool', dst[:, 0:128], kk_[s * DS:(s + 1) * DS, :])
                            S.dma('pool', dst[:, 128:256], vv_[s * DS:(s + 1) * DS, :])
                    elif tok0 >= T - w:
                        r0 = tok0 - (T - w)
                        S.dma('pool', kv_o[gi][l, 0, r0:r0 + 128, 0:128], kk_[:, :])
                        S.dma('pool', kv_o[gi][l, 0, r0:r0 + 128, 128:256], vv_[:, :])
        S.flush()

    with ExitStack() as ph:
        sb = lambda n, s, d=F32: ph.enter_context(nc.sbuf_tensor(f"{n}_L{l}", s, d))
        wst = sb("pa_wst", [128, 2, 128])
        wblk = sb("pa_w", [128, 2, 128], BF)
        S.dma('sp', wst[:], E['pool_wblk'][l].rearrange("c p m -> p c m"))
        S.cp('dve', wblk[:], wst[:])
        scl = sb("pa_scl", [128, 2])
        S.dma('sp', scl[:], E['pool_scale'][l])
        inv = sb("pa_inv", [128, 2, 2, 512])
        S.dma('sp', inv[:], E['c_invcnt'])
        extp = Rot(nc, ph, "pa_ext", [128, 2, 527], F32, 2)
        s2p = Rot(nc, ph, "pa_s2", [128, 2, 527], F32, 1)
        s4p = Rot(nc, ph, "pa_s4", [128, 2, 527], F32, 1)
        s8p = Rot(nc, ph, "pa_s8", [128, 2, 527], F32, 1)
        s16p = Rot(nc, ph, "pa_s16", [128, 2, 527], F32, 1)
        mp = Rot(nc, ph, "pa_m", [128, 2, 512], F32, 1)
        dp = Rot(nc, ph, "pa_d", [128, 2, 512], BF, 2)
        yp = Rot(nc, ph, "pa_y", [128, 2, 512], BF, 2)
        zpv = zpT
        ypv = ypT

        def pool_core(ext, n, which, lead):
            sl = lambda a, b: (slice(None),) * (2 + lead) + (slice(a, b),)
            s2, s4, s8, s16 = s2p.nxt(), s4p.nxt(), s8p.nxt(), s16p.nxt()
            return s2, s4, s8, s16

        for ti, (t0, n) in enumerate(tiles):
            S.maybe_flush()
            sample = (t0 == T)
            ext = extp.nxt()
            if not sample:
                if ti == 0:
                    S.ms('pool', ext[:, :, 0:15], 0.0)
                    S.dma('sp', ext[:, :, 15:15 + n], zpv[:, :, 0:n])
                else:
                    S.dma('sp', ext[:, :, 0:15 + n], zpv[:, :, t0 - 15:t0 + n])
                E_ = ext[:, :, 0:15 + n]
                W_ = 15 + n
                shp = lambda a, lo, hi: a[:, :, lo:hi]
                nn = n
                M3 = lambda a: a[:, :, 0:n]
                invv = inv[:, 0 if ti == 0 else 1, :, 0:n]
            else:
                hst = sb("pa_hst", [15, NSQ, 256])
                S.dma('sp', hst[:], E['st_pool'][l].rearrange("s t c -> t s c"))
                e4 = ext[:, :, 0:NSQ * 23].rearrange("p c (s w) -> p c s w", w=23)
                for s in range(NSQ):
                    ps = PS.nxt()
                    for c in range(2):
                        S.tr(ps[:, c * 16:c * 16 + 15], hst[0:15, s, c * 128:(c + 1) * 128], ident[0:15, 0:15])
                    S.cp('dve', e4[:, :, s, 0:15], ps[:, 0:32].rearrange("p (c w) -> p c w", w=16)[:, :, 0:15])
                for c in range(2):
                    S.dma('sp', e4[:, c, :, 15:23], zpv[:, c, T:T + TS].rearrange("p (s w) -> p s w", w=DS))
                nn = DS
                shp = None
                invv = None
            s2, s4, s8, s16, m_, d_ = s2p.nxt(), s4p.nxt(), s8p.nxt(), s16p.nxt(), mp.nxt(), dp.nxt()
            if not sample:
                V = lambda a, lo, hi: a[:, :, lo:hi]
                O = lambda a: a[:, :, 0:n]
                IV = lambda c, p0: inv[p0:p0 + 64, 0 if ti == 0 else 1, c, 0:n]
                P = lambda a, c, p0, lo, hi: a[p0:p0 + 64, c, lo:hi]
                U = ext[:, :, 15:15 + n]
            else:
                v4 = lambda a: a[:, :, 0:NSQ * 23].rearrange("p c (s w) -> p c s w", w=23)
                V = lambda a, lo, hi: v4(a)[:, :, :, lo:hi]
                o4 = lambda a: a[:, :, 0:TS].rearrange("p c (s w) -> p c s w", w=DS)
                O = o4
                IV = lambda c, p0: inv[p0:p0 + 64, 1, c, 0:TS].rearrange("p (s w) -> p s w", w=DS)
                P = lambda a, c, p0, lo, hi: v4(a)[p0:p0 + 64, c, :, lo:hi]
                U = v4(ext)[:, :, :, 15:23]
            W_ = 15 + nn
            S.tt('dve', V(s2, 0, W_ - 1), V(ext, 0, W_ - 1), V(ext, 1, W_), ALU.add)
            S.tt('pool', V(s4, 0, W_ - 3), V(s2, 0, W_ - 3), V(s2, 2, W_ - 1), ALU.add)
            S.tt('dve', V(s8, 0, W_ - 7), V(s4, 0, W_ - 7), V(s4, 4, W_ - 3), ALU.add)
            S.tt('pool', V(s16, 0, W_ - 15), V(s8, 0, W_ - 15), V(s8, 8, W_ - 7), ALU.add)
            Om = O(m_)
            for gi4, (src, off) in enumerate(((s2, 14), (s4, 12), (s8, 8), (s16, 0))):
                c, p0 = gi4 // 2, (gi4 % 2) * 64
                if not sample:
                    dstm = m_[p0:p0 + 64, c, 0:n]
                else:
                    dstm = o4(m_)[p0:p0 + 64, c, :, :]
                S.tt('dve' if gi4 % 2 == 0 else 'pool', dstm, P(src, c, p0, off, off + nn), IV(c, p0), ALU.mult)
            S.tt('dve', O(d_), O(m_), U, ALU.subtract)
            y_ = yp.nxt()
            for c in range(2):
                ps = PS.nxt()
                S.mm(ps[:, 0:n], wblk[:, c, :], d_[:, c, 0:n])
                S.ts('dve', y_[:, c, 0:n], ps[:, 0:n], scl[:, c:c + 1], None, ALU.mult)
            S.dma('pool', ypv[:, :, t0:t0 + n], y_[:, :, 0:n])
        S.flush()

    attention(nc, S, E, l)
    rwkv(nc, S, E, l)
    mix_out(nc, S, E, l)
    ffn_ple(nc, S, E, l)


def attention(nc, S, E, l):
    T, TT = E['T'], E['TT']
    PS, PSB = E['PS'], E['PSB']
    ident, identb, masksb = E['ident'], E['identb'], E['masksb']
    qk_tok, v_tok, og, yaT = E['qk_tok'], E['v_tok'], E['og'], E['yaT']
    with ExitStack() as ph:
        sb = lambda n, s, d=F32: ph.enter_context(nc.sbuf_tensor(f"{n}_L{l}", s, d))
        qkp = Rot(nc, ph, "at_qk", [128, 2, 128], BF, 3)
        vp = Rot(nc, ph, "at_v", [128, 2, 65], BF, 4)
        qTp = Rot(nc, ph, "at_qT", [128, 128], BF, 2)
        kTp = Rot(nc, ph, "at_kT", [128, 128], BF, 4)
        pp = Rot(nc, ph, "at_p", [128, 128], BF, 4)
        pmp = Rot(nc, ph, "at_pm", [128, 128], BF, 4)
        op_ = Rot(nc, ph, "at_o", [128, 2, 65], F32, 3)
        import os
        for gi, (w, dil) in enumerate(GROUPS):
            if os.environ.get("GRP") and str(gi) not in os.environ["GRP"]:
                continue
            Ls = T // dil
            qv = qk_tok[0:T, :].rearrange("(l d) c -> d l c", d=dil)
            vv = v_tok[0:T, :, :].rearrange("(l d) h c -> d l h c", d=dil)
            ov = og[gi].rearrange("(l d) h c -> d l h c", d=dil)
            for r in range(dil):
                kT_prev = None
                v_prev = None
                for b in range(Ls // 128):
                    S.maybe_flush()
                    qk = qkp.nxt()
                    S.dma('sp', qk[:, 0, :], qv[r, b * 128:(b + 1) * 128, gi * 128:(gi + 1) * 128])
                    S.dma('sp', qk[:, 1, :], qv[r, b * 128:(b + 1) * 128, 384 + gi * 128:384 + (gi + 1) * 128])
                    v_ = vp.nxt()
                    S.dma('act', v_[:], vv[r, b * 128:(b + 1) * 128, 2 * gi:2 * gi + 2, :])
                    pt = PSB.nxt()
                    S.tr(pt[:, 0:128], qk[:, 0, :], identb[:])
                    S.tr(pt[:, 128:256], qk[:, 1, :], identb[:])
                    qT, kT = qTp.nxt(), kTp.nxt()
                    S.cp('act', qT[:], pt[:, 0:128])
                    S.cp('dve', kT[:], pt[:, 128:256])
                    po = E['PA']
                    for hh in range(2):
                        hs = slice(hh * 64, hh * 64 + 64)
                        srcs = [(kT, v_, 3)] + ([(kT_prev, v_prev, 2)] if b > 0 else [])
                        for si, (kt_, vt_, mi) in enumerate(srcs):
                            pss = PS.nxt()
                            S.mm(pss[:, 0:128], kt_[hs, :], qT[hs, :])
                            p_ = pp.nxt()
                            S.act(p_[:], pss[:, 0:128], AF.Exp, scale=0.125)
                            pm = pmp.nxt()
                            S.tt('pool' if si == 0 else 'dve', pm[:], p_[:], masksb[:, mi - 1 if False else (2 if si == 0 else 3), :], ALU.mult)
                            S.mm(po[:, hh * 65:hh * 65 + 65], pm[:], vt_[:, hh, :], start=(si == 0), stop=(si == len(srcs) - 1))
                    o_ = op_.nxt()
                    S.cp('act', o_[:], po[:, 0:130].rearrange("p (h c) -> p h c", c=65))
                    S.dma('pool', ov[r, b * 128:(b + 1) * 128, :, :], o_[:])
                    kT_prev, v_prev = kT, v_
        S.flush()
        o3 = Rot(nc, ph, "at_o3", [128, 3, 130], F32, 2)
        ysp = Rot(nc, ph, "at_ys", [128, 2, 65], F32, 2)
        rcp = Rot(nc, ph, "at_rc", [128, 2, 1], F32, 2)
        ybp = Rot(nc, ph, "at_yb", [128, 2, 64], BF, 2)
        yTp = Rot(nc, ph, "at_yT", [128, 128], BF, 2)

        def finish(ys, m, col0):
            rc = rcp.nxt()
            S.add('dve', lambda e: e.reciprocal(rc[0:m], ys[0:m, :, 64:65]), [ys.name], [rc.name])
            yb = ybp.nxt()
            S.tt('dve', yb[0:m], ys[0:m, :, 0:64], rc[0:m].broadcast_to([m, 2, 64]), ALU.mult)
            pt = PSB.nxt()
            S.tr(pt[:, 0:m], yb[0:m].rearrange("p h c -> p (h c)"), identb[0:m, 0:m])
            yT = yTp.nxt()
            S.cp('act', yT[:, 0:m], pt[:, 0:m])
            S.dma('pool', yaT[:, col0:col0 + m], yT[:, 0:m])

        for b in range(T // 128):
            S.maybe_flush()
            o_ = o3.nxt()
            for gi in range(3):
                S.dma('sp' if gi != 1 else 'act', o_[:, gi, :], og[gi][b * 128:(b + 1) * 128].rearrange("t h c -> t (h c)"))
            ys = ysp.nxt()
            ysf = ys[:].rearrange("p h c -> p (h c)")
            S.tt('dve', ysf, o_[:, 0, :], o_[:, 1, :], ALU.add)
            S.tt('dve', ysf, ysf, o_[:, 2, :], ALU.add)
            finish(ys, 128, b * 128)
        cst = Rot(nc, ph, "as_c", [128, 256], F32, 3)
        kbp = Rot(nc, ph, "as_kb", [128, 128], BF, 3)
        vbp = [sb(f"as_vb{i}", [128, 2, 65], BF) for i in range(3)]
        for t_ in vbp:
            S.ms('pool', t_[:], 1.0)
        sm = sb("as_sm", [128, 21, 8])
        smb = sb("as_smb", [128, 21, 8], BF)
        smn = sb("as_smn", [8, 3, 8])
        smnb = sb("as_smnb", [8, 3, 8], BF)
        S.dma('sp', sm[:], E['c_smask'])
        S.dma('sp', smn[:], E['c_smask_new'])
        S.cp('dve', smb[:], sm[:])
        S.cp('dve', smnb[:], smn[:])
        qs = sb("as_q", [TS, 768], BF)
        S.dma('sp', qs[:], qk_tok[T:T + TS, :])
        vs = sb("as_v", [DS, NSQ, 6, 65], BF)
        S.dma('sp', vs[:].rearrange("t s h c -> t s (h c)"), v_tok[T:T + TS].rearrange("(s t) h c -> t s (h c)", t=DS))
        qTs = sb("as_qT", [128, 6, TS], BF)
        for j in range(6):
            pt = PSB.nxt()
            S.tr(pt[:, 0:TS], qs[:, j * 128:(j + 1) * 128], identb[0:TS, 0:TS])
            S.cp('act', qTs[:, j, :], pt[:, 0:TS])
        psp = Rot(nc, ph, "as_p", [128, 8], BF, 4)
        pmsp = Rot(nc, ph, "as_pm", [128, 8], BF, 4)
        vi = 0
        accp = Rot(nc, ph, "as_acc", [8, 130], F32, 2)
        for s in range(NSQ):
            acc = accp.nxt()
            S.ms('dve', acc[:], 0.0)
            started = [False, False]
            bi = 0
            for gi, (w, dil) in enumerate(GROUPS):
                blist = [(bb, 128) for bb in range(w // 128)] + [(-1, DS)]
                for (bb, m) in blist:
                    S.maybe_flush()
                    last = (gi == 2 and bb == -1)
                    if bb >= 0:
                        c_ = cst.nxt()
                        S.dma('sp' if bb % 2 == 0 else 'act', c_[:], E['cache'][gi][l, s, bb * 128:(bb + 1) * 128, :])
                        kb = kbp.nxt()
                        S.cp('pool', kb[:], c_[:, 0:128])
                        vb_ = vbp[vi % 3]
                        vi += 1
                        S.cp('pool', vb_[:, :, 0:64], c_[:, 128:256].rearrange("p (h c) -> p h c", c=64))
                        pt = PSB.nxt()
                        S.tr(pt[:, 0:128], kb[:], identb[:])
                        kT = kbp.nxt()
                        S.cp('act', kT[:], pt[:, 0:128])
                        mk = smb[:, bi, :]
                        bi += 1
                    po = E['PA']
                    for hh in range(2):
                        hs = slice(hh * 64, hh * 64 + 64)
                        pss = PS.nxt()
                        if bb >= 0:
                            S.mm(pss[0:m, 0:8], kT[hs, :], qTs[hs, gi, s * DS:(s + 1) * DS])
                        else:
                            S.mm(pss[0:m, 0:8], qTs[hs, 3 + gi, s * DS:(s + 1) * DS], qTs[hs, gi, s * DS:(s + 1) * DS])
                        p_ = psp.nxt()
                        S.act(p_[0:m], pss[0:m, 0:8], AF.Exp, scale=0.125)
                        pm = pmsp.nxt()
                        S.tt('dve', pm[0:m], p_[0:m], mk if bb >= 0 else smnb[:, gi, :], ALU.mult)
                        rhs = vb_[:, hh, :] if bb >= 0 else vs[:, s, 2 * gi + hh, :]
                        S.mm(po[0:8, hh * 65:hh * 65 + 65], pm[0:m], rhs, start=True, stop=True)
                    S.tt('dve', acc[:], acc[:], po[0:8, 0:130], ALU.add)
            ys = ysp.nxt()
            S.cp('act', ys[0:8], acc[:].rearrange("p (h c) -> p h c", c=65))
            finish(ys, 8, T + s * DS)
        S.flush()


def rwkv(nc, S, E, l):
    T, TT = E['T'], E['TT']
    PS = E['PS']
    ident, blkones, masks = E['ident'], E['blkones'], E['masks']
    zrT, yrT, shift_o, wkv_o = E['zrT'], E['yrT'], E['shift_o'], E['wkv_o']
    NR = 256
    tiles = [(i * NR, NR) for i in range(T // NR)] + [(T, TS)]
    zrv = zrT
    yrv = yrT
    with ExitStack() as ph:
        sb = lambda n, s, d=F32: ph.enter_context(nc.sbuf_tensor(f"{n}_L{l}", s, d))
        mu = sb("rw_mu", [128, 11])
        S.dma('sp', mu[:], E['rw_mu'][l])
        v3 = sb("rw_v3", [128, 15])
        S.dma('sp', v3[:], E['rw_vec3'][l])
        w0, a0, k_k, k_a, r_k = (v3[:, i * 3:(i + 1) * 3] for i in range(5))
        lst = sb("rw_lst", [128, 2, 384])
        S.dma('sp', lst[:, 0, :], E['rw_w2a2'][l])
        S.dma('sp', lst[:, 1, :], E['rw_g2'][l])
        w2a2 = sb("rw_w2a2b", [128, 384], BF)
        g2 = sb("rw_g2b", [128, 384], BF)
        S.cp('dve', w2a2[:], lst[:, 0, :])
        S.cp('dve', g2[:], lst[:, 1, :])
        lnb = sb("rw_lnb", [128, 2, 384])
        S.dma('sp', lnb[:, 0, :], E['rw_ln'][l, 0].partition_broadcast(128))
        S.dma('sp', lnb[:, 1, :], E['rw_ln'][l, 1].partition_broadcast(128))
        mub = sb("rw_mub", [128, 11, NR])
        S.ms('dve', mub[:], 1.0)
        for j in range(11):
            S.ts('dve' if j % 2 else 'pool', mub[:, j, :], mub[:, j, :], mu[:, j:j + 1], None, ALU.mult)
        rst = sb("rw_rst", [128, 2, 512])
        S.dma('sp', rst[:], E['c_reset'])
        ident64 = ident[0:64, 0:64]
        Hs = [sb(f"rw_H{i}", [64, 6, 64]) for i in range(2)]
        zc = Rot(nc, ph, "rw_zc", [128, 11, NR + 1], F32, 1)
        zsT = sb("rw_zs", [128, 11, NR])
        dzT = sb("rw_dz", [128, 11, NR])
        lora = sb("rw_lora", [128, NR], BF)
        sgz = sb("rw_sgz", [128, NR], BF)
        sg = sb("rw_sg", [128, 3, NR])
        aa = sb("rw_a", [128, 3, NR])
        gg = sb("rw_g", [128, 3, NR])
        kk = sb("rw_kk", [128, 3, NR])
        kk2 = sb("rw_kk2", [128, 3, NR], BF)
        kp = sb("rw_kp", [128, 3, NR])
        tmpa = sb("rw_tmpa", [128, 3, NR])
        tmpb = sb("rw_tmpb", [128, 3, NR], BF)
        bon = sb("rw_bon", [128, 3, NR])
        cum = sb("rw_cum", [128, 3, NR])
        G = sb("rw_G", [128, 3, NR])
        Gi = sb("rw_Gi", [128, 3, NR])
        Gm = sb("rw_Gm", [128, 3, NR])
        at = sb("rw_at", [128, 3, NR])
        bt = sb("rw_bt", [128, 3, NR])
        kt = sb("rw_kt", [128, 3, NR])
        rt = sb("rw_rt", [128, 3, NR])
        tokp = Rot(nc, ph, "rw_tok", [128, 5, 384], F32, 2)
        Np = Rot(nc, ph, "rw_N", [128, 128], F32, 3)
        Lp = Rot(nc, ph, "rw_L", [128, 128], F32, 3)
        Ak = Rot(nc, ph, "rw_Ak", [128, 128], F32, 2)
        Arb = Rot(nc, ph, "rw_Arb", [128, 128], F32, 2)
        Ark = Rot(nc, ph, "rw_Ark", [128, 128], F32, 2)
        Xp = Rot(nc, ph, "rw_X", [128, 128], F32, 3)
        Pm = Rot(nc, ph, "rw_Pm", [64, 64], F32, 2)
        Qg = Rot(nc, ph, "rw_Qg", [64, 64], F32, 2)
        YH = Rot(nc, ph, "rw_YH", [64, 128], F32, 2)
        Yt = Rot(nc, ph, "rw_Y", [128, 384], F32, 2)
        stp = Rot(nc, ph, "rw_st", [128, 6], F32, 2)
        st2 = Rot(nc, ph, "rw_st2", [128, 6], F32, 2)
        ysq = Rot(nc, ph, "rw_ysq", [128, 384], F32, 1)
        ynT = sb("rw_ynT", [128, 3, NR])
        yo = Rot(nc, ph, "rw_yo", [128, 3, NR], BF, 2)
        wst = sb("rw_wst", [64, 6, 64])
        wso = Rot(nc, ph, "rw_wso", [64, 6, 64], F32, 2)
        hcur = 0
        S.ms('dve', Hs[0][:], 0.0)

        def chunk(c0, C, lvls, Hin, Hout, y_dst_col):
            tok = tokp.nxt()
            for qi, src in enumerate((at, bt, kt, zsT, rt)):
                for j in range(3):
                    ps = PS.nxt()
                    sv = src[:, 6 + j, c0:c0 + C] if src is zsT else src[:, j, c0:c0 + C]
                    S.tr(ps[0:C, 0:128], sv, ident[:])
                    S.cp('act' if (qi + j) % 2 == 0 else 'dve', tok[0:C, qi, j * 128:(j + 1) * 128], ps[0:C, 0:128])
            Y = Yt.nxt()
            for h in range(6):
                j, po = h // 2, (h % 2) * 64
                fs = lambda a: a[po:po + 64, j, c0:c0 + C]
                ts_ = lambda qi: tok[0:C, qi, h * 64:(h + 1) * 64]
                aT, bT, kT_, vT, rT = (ts_(i) for i in range(5))
                N0, L0, AkT, ArbT, ArkT = Np.nxt(), Lp.nxt(), Ak.nxt(), Arb.nxt(), Ark.nxt()
                for dst, lh, rh, mi, eng in ((N0, bt, at, 0, 'dve'), (L0, at, bt, 1, 'dve'), (AkT, kt, at, 0, 'dve'),
                                             (ArbT, bt, rt, 2, 'dve'), (ArkT, kt, rt, 2, 'dve')):
                    ps = PS.nxt()
                    S.mm(ps[0:C, 0:C], fs(lh), fs(rh))
                    S.tt(eng, dst[0:C, 0:C], ps[0:C, 0:C], masks[0:C, mi, 0:C], ALU.mult)
                X = Xp.nxt()
                S.cp('pool', X[0:C, 0:64], aT)
                ps = PS.nxt()
                S.mm(ps[0:C, 0:64], AkT[0:C, 0:C], vT)
                S.cp('act', X[0:C, 64:128], ps[0:C, 0:64])
                Nc, Lc = N0, L0
                for lv in range(lvls):
                    ps = PS.nxt()
                    S.mm(ps[0:C, 0:128], Nc[0:C, 0:C], X[0:C, :])
                    Xn = Xp.nxt()
                    S.tt('dve', Xn[0:C, :], ps[0:C, 0:128], X[0:C, :], ALU.add)
                    X = Xn
                    if lv < lvls - 1:
                        ps1, ps2 = PS.nxt(), PS.nxt()
                        S.mm(ps1[0:C, 0:C], Nc[0:C, 0:C], Lc[0:C, 0:C])
                        S.mm(ps2[0:C, 0:C], Lc[0:C, 0:C], Nc[0:C, 0:C])
                        Ln, Nn = Lp.nxt(), Np.nxt()
                        S.cp('act', Ln[0:C, 0:C], ps1[0:C, 0:C])
                        S.cp('act', Nn[0:C, 0:C], ps2[0:C, 0:C])
                        Nc, Lc = Nn, Ln
                Wm, U0 = X[0:C, 0:64], X[0:C, 64:128]
                gC = G[po:po + 64, j, c0 + C - 1:c0 + C]
                ps = PS.nxt()
                S.mm(ps[0:64, 0:64], Wm, bT)
                pm_ = Pm.nxt()
                S.tt('dve', pm_[:], ps[0:64, 0:64], ident64, ALU.add)
                ps = PS.nxt()
                S.mm(ps[0:64, 0:64], bT, U0, start=True, stop=False)
                S.mm(ps[0:64, 0:64], kT_, vT, start=False, stop=True)
                qg = Qg.nxt()
                if po == 0:
                    S.ts('dve', qg[:], ps[0:64, 0:64], gC, None, ALU.mult)
                else:
                    gc0 = stp.nxt()
                    S.cp('pool', gc0[0:64, 0:1], gC)
                    gC = gc0[0:64, 0:1]
                    S.ts('dve', qg[:], ps[0:64, 0:64], gC, None, ALU.mult)
                ps = PS.nxt()
                S.mm(ps[0:64, 0:C], Wm, ArbT[0:C, 0:C], start=True, stop=False)
                S.mm(ps[0:64, 0:C], rT, ident[0:C, 0:C], start=False, stop=True)
                yh = YH.nxt()
                S.cp('act', yh[:, 0:C], ps[0:64, 0:C])
                ps = PS.nxt()
                S.mm(ps[0:C, 0:64], ArbT[0:C, 0:C], U0, start=True, stop=False)
                S.mm(ps[0:C, 0:64], ArkT[0:C, 0:C], vT, start=False, stop=False)
                S.mm(ps[0:C, 0:64], yh[:, 0:C], Hin[:, h, :], start=False, stop=True)
                S.cp('act', Y[0:C, h * 64:(h + 1) * 64], ps[0:C, 0:64])
                ps = PS.nxt()
                S.mm(ps[0:64, 0:64], pm_[:], Hin[:, h, :])
                S.stt('dve', Hout[:, h, :], ps[0:64, 0:64], gC, qg[:], ALU.mult, ALU.add)
            s1, s2_ = stp.nxt(), st2.nxt()
            Y3 = Y[0:C, :].rearrange("p (h c) -> p h c", c=64)
            S.red('dve', s1[0:C, :], Y3, ALU.add)
            sq = ysq.nxt()
            S.tt('pool', sq[0:C, :], Y[0:C, :], Y[0:C, :], ALU.mult)
            S.red('dve', s2_[0:C, :], sq[0:C, :].rearrange("p (h c) -> p h c", c=64), ALU.add)
            S.ts('dve', s1[0:C, :], s1[0:C, :], 1.0 / 64, None, ALU.mult)
            S.ts('dve', s2_[0:C, :], s2_[0:C, :], 1.0 / 64, None, ALU.mult)
            m2 = stp.nxt()
            S.tt('dve', m2[0:C, :], s1[0:C, :], s1[0:C, :], ALU.mult)
            S.tt('dve', s2_[0:C, :], s2_[0:C, :], m2[0:C, :], ALU.subtract)
            S.ts('dve', s2_[0:C, :], s2_[0:C, :], 64e-5, None, ALU.add)
            S.act(s2_[0:C, :], s2_[0:C, :], AF.Ln)
            S.act(s2_[0:C, :], s2_[0:C, :], AF.Exp, scale=-0.5)
            S.tt('dve', Y3, Y3, s1[0:C, :].unsqueeze(2).broadcast_to([C, 6, 64]), ALU.subtract)
            S.tt('dve', Y3, Y3, s2_[0:C, :].unsqueeze(2).broadcast_to([C, 6, 64]), ALU.mult)
            S.tt('pool', Y[0:C, :], Y[0:C, :], lnb[0:C, 0, :], ALU.mult)
            S.tt('pool', Y[0:C, :], Y[0:C, :], lnb[0:C, 1, :], ALU.add)
            for j in range(3):
                ps = PS.nxt()
                S.tr(ps[:, 0:C], Y[0:C, j * 128:(j + 1) * 128], ident[0:C, 0:C])
                S.cp('act', ynT[:, j, y_dst_col:y_dst_col + C], ps[:, 0:C])

        for ti, (t0, n) in enumerate(tiles):
            S.maybe_flush()
            sample = (t0 == T)
            z = zc.nxt()
            if not sample:
                if ti == 0:
                    S.ms('pool', z[:, :, 0:1], 0.0)
                    S.dma('sp', z[:, :, 1:1 + n], zrv[:, :, 0:n])
                else:
                    S.dma('sp', z[:, 0:6, 0:1 + n], zrv[:, 0:6, t0 - 1:t0 + n])
                    S.dma('act', z[:, 6:11, 0:1 + n], zrv[:, 6:11, t0 - 1:t0 + n])
                prev = z[:, :, 0:n]
            else:
                S.dma('sp', z[:, :, 1:1 + n], zrv[:, :, T:T + n])
                pv = dzT[:, :, 0:n]
                pv4 = pv.rearrange("p c (s w) -> p c s w", w=DS)
                S.cp('pool', pv4[:, :, :, 1:DS], z[:, :, 1:1 + n].rearrange("p c (s w) -> p c s w", w=DS)[:, :, :, 0:DS - 1])
                for s in range(NSQ):
                    S.dma('sp', pv4[:, :, s, 0], E['st_shift'][l, s].rearrange("(c p) -> p c", p=128), slow=True)
                prev = pv
            zcur = z[:, :, 1:1 + n]
            dz = dzT[:, :, 0:n]
            zs = zsT[:, :, 0:n]
            S.tt('dve', dz, prev, zcur, ALU.subtract)
            S.tt('pool', dz, dz, mub[:, :, 0:n], ALU.mult)
            S.tt('dve', zs, dz, zcur, ALU.add)
            r_, k_, v_ = zs[:, 0:3, :], zs[:, 3:6, :], zs[:, 6:9, :]
            S.act(lora[0:64, 0:n], zs[0:64, 9, :], AF.Tanh)
            S.cp('pool', lora[64:128, 0:n], zs[64:128, 9, :])
            S.act(sgz[:, 0:n], zs[:, 10, :], AF.Sigmoid)
            for j in range(3):
                cs_ = slice(j * 128, (j + 1) * 128)
                ps = PS.nxt()
                S.mm(ps[:, 0:n], w2a2[0:64, cs_], lora[0:64, 0:n])
                S.act(sg[:, j, 0:n], ps[:, 0:n], AF.Sigmoid, bias=w0[:, j:j + 1])
                ps = PS.nxt()
                S.mm(ps[:, 0:n], w2a2[64:128, cs_], lora[64:128, 0:n])
                S.act(aa[:, j, 0:n], ps[:, 0:n], AF.Sigmoid, bias=a0[:, j:j + 1])
                ps = PS.nxt()
                S.mm(ps[:, 0:n], g2[:, cs_], sgz[:, 0:n])
                S.cp('act', gg[:, j, 0:n], ps[:, 0:n])
                S.ts('pool', kk[:, j, 0:n], k_[:, j, :], k_k[:, j:j + 1], None, ALU.mult)
                S.tt('pool', kk2[:, j, 0:n], kk[:, j, 0:n], kk[:, j, 0:n], ALU.mult)
                ps = PS.nxt()
                S.mm(ps[:, 0:n], blkones[:], kk2[:, j, 0:n])
                S.ts('dve', tmpa[:, j, 0:n], ps[:, 0:n], 1e-24, None, ALU.max)
                S.act(tmpa[:, j, 0:n], tmpa[:, j, 0:n], AF.Ln)
                S.act(tmpa[:, j, 0:n], tmpa[:, j, 0:n], AF.Exp, scale=-0.5)
                S.tt('dve', kk[:, j, 0:n], kk[:, j, 0:n], tmpa[:, j, 0:n], ALU.mult)
                S.ts('dve', tmpa[:, j, 0:n], aa[:, j, 0:n], -1.0, k_a[:, j:j + 1], ALU.add, ALU.mult)
                S.stt('dve', kp[:, j, 0:n], tmpa[:, j, 0:n], 1.0, k_[:, j, :], ALU.add, ALU.mult)
                S.stt('pool', tmpb[:, j, 0:n], r_[:, j, :], r_k[:, j:j + 1], kp[:, j, 0:n], ALU.mult, ALU.mult)
                ps = PS.nxt()
                S.mm(ps[:, 0:n], blkones[:], tmpb[:, j, 0:n])
                S.tt('dve', bon[:, j, 0:n], ps[:, 0:n], v_[:, j, :], ALU.mult)
                S.scan(cum[:, j, 0:n], rst[:, 1 if sample else 0, 0:n], sg[:, j, 0:n], 0.0, ALU.mult, ALU.add)
            S.act(G[:, :, 0:n], cum[:, :, 0:n], AF.Exp, scale=-C0)
            S.act(Gi[:, :, 0:n], cum[:, :, 0:n], AF.Exp, scale=C0)
            S.tt('pool', tmpa[:, :, 0:n], cum[:, :, 0:n], sg[:, :, 0:n], ALU.subtract)
            S.act(Gm[:, :, 0:n], tmpa[:, :, 0:n], AF.Exp, scale=-C0)
            S.tt('dve', rt[:, :, 0:n], r_, G[:, :, 0:n], ALU.mult)
            S.tt('pool', kt[:, :, 0:n], kp[:, :, 0:n], Gi[:, :, 0:n], ALU.mult)
            S.tt('dve', tmpa[:, :, 0:n], kk[:, :, 0:n], aa[:, :, 0:n], ALU.mult)
            S.tt('dve', bt[:, :, 0:n], tmpa[:, :, 0:n], Gi[:, :, 0:n], ALU.mult)
            S.stt('pool', at[:, :, 0:n], kk[:, :, 0:n], -1.0, Gm[:, :, 0:n], ALU.mult, ALU.mult)
            if not sample:
                for c in range(n // 128):
                    S.maybe_flush()
                    chunk(c * 128, 128, 7, Hs[hcur], Hs[1 - hcur], c * 128)
                    hcur = 1 - hcur
                if t0 + n == T:
                    wo = wso.nxt()
                    for h in range(6):
                        ps = PS.nxt()
                        S.tr(ps[0:64, 0:64], Hs[hcur][:, h, :], ident[0:64, 0:64])
                        S.cp('act', wo[:, h, :], ps[0:64, 0:64])
                    S.dma('pool', wkv_o[l, 0].rearrange("h v k -> v h k"), wo[:])
            else:
                for s in range(NSQ):
                    S.dma('sp', wst[:], E['st_wkv'][l, s].rearrange("h v k -> v h k"))
                    Hi, Ho = Hs[0], Hs[1]
                    for h in range(6):
                        ps = PS.nxt()
                        S.tr(ps[0:64, 0:64], wst[:, h, :], ident[0:64, 0:64])
                        S.cp('act', Hi[:, h, :], ps[0:64, 0:64])
                    chunk(s * DS, DS, 3, Hi, Ho, s * DS)
                    wo = wso.nxt()
                    for h in range(6):
                        ps = PS.nxt()
                        S.tr(ps[0:64, 0:64], Ho[:, h, :], ident[0:64, 0:64])
                        S.cp('act', wo[:, h, :], ps[0:64, 0:64])
                    S.dma('pool', wkv_o[l, 1 + s].rearrange("h v k -> v h k"), wo[:])
            y_ = yo.nxt()
            S.tt('dve', ynT[:, :, 0:n], ynT[:, :, 0:n], bon[:, :, 0:n], ALU.add)
            S.tt('pool', y_[:, :, 0:n], ynT[:, :, 0:n], gg[:, :, 0:n], ALU.mult)
            S.dma('pool', yrv[:, :, t0:t0 + n], y_[:, :, 0:n])
        S.flush()


def mix_out(nc, S, E, l):
    T, TT = E['T'], E['TT']
    PS = E['PS']
    xT, gT, ypT, yaT, yrT = E['xT'], E['gT'], E['ypT'], E['yaT'], E['yrT']
    tiles = [(i * 512, 512) for i in range(T // 512)] + [(T, TS)]
    xTv = xT
    gTv = gT
    with ExitStack() as ph:
        sb = lambda n, s, d=F32: ph.enter_context(nc.sbuf_tensor(f"{n}_L{l}", s, d))
        stage = Rot(nc, ph, "mo_st", [128, D], F32, 3)
        wproj = sb("mo_wp", [128, 6, D], BF)
        wo = sb("mo_wo", [128, KC, D], BF)
        load_cast(S, stage, wproj[:, 0:2, :], E['proj_pool'][l], 2)
        load_cast(S, stage, wproj[:, 2:3, :], E['proj_attn'][l], 1)
        load_cast(S, stage, wproj[:, 3:6, :], E['proj_rwkv'][l], 3)
        load_cast(S, stage, wo, E['w_out'][l], KC)
        yp = Rot(nc, ph, "mo_y", [128, 6, 512], BF, 2)
        gp = Rot(nc, ph, "mo_g", [128, 24, 512], BF, 2)
        xp = Rot(nc, ph, "mo_x", [128, KC, 512], F32, 2)
        mg = Rot(nc, ph, "mo_mg", [128, KC, 512], BF, 2)
        m1p = Rot(nc, ph, "mo_m1", [128, 512], F32, 2)
        m2p = Rot(nc, ph, "mo_m2", [128, 512], F32, 2)
        for (t0, n) in tiles:
            S.maybe_flush()
            y_ = yp.nxt()
            S.dma('sp', y_[:, 0:2, 0:n], ypT[:, :, t0:t0 + n])
            S.dma('sp', y_[:, 2, 0:n], yaT[:, t0:t0 + n])
            S.dma('sp', y_[:, 3:6, 0:n], yrT[:, :, t0:t0 + n])
            g_ = gp.nxt()
            S.dma('act', g_[:, :, 0:n], gTv[:, :, t0:t0 + n])
            x = xp.nxt()
            S.dma('sp', x[:, :, 0:n], xTv[:, :, t0:t0 + n])
            m_ = mg.nxt()
            for oc in range(KC):
                cs_ = slice(oc * 128, (oc + 1) * 128)
                ps = PS.nxt()
                for c in range(2):
                    S.mm(ps[:, 0:n], wproj[:, c, cs_], y_[:, c, 0:n], start=(c == 0), stop=(c == 1))
                m1 = m1p.nxt()
                S.tt('dve', m1[:, 0:n], ps[:, 0:n], g_[:, oc, 0:n], ALU.mult)
                ps = PS.nxt()
                S.mm(ps[:, 0:n], wproj[:, 2, cs_], y_[:, 2, 0:n])
                m2 = m2p.nxt()
                S.tt('dve', m2[:, 0:n], ps[:, 0:n], g_[:, 8 + oc, 0:n], ALU.mult)
                S.tt('pool', m1[:, 0:n], m1[:, 0:n], m2[:, 0:n], ALU.add)
                ps = PS.nxt()
                for c in range(3):
                    S.mm(ps[:, 0:n], wproj[:, 3 + c, cs_], y_[:, 3 + c, 0:n], start=(c == 0), stop=(c == 2))
                m2 = m2p.nxt()
                S.tt('dve', m2[:, 0:n], ps[:, 0:n], g_[:, 16 + oc, 0:n], ALU.mult)
                S.tt('pool', m_[:, oc, 0:n], m1[:, 0:n], m2[:, 0:n], ALU.add)
            for oc in range(KC):
                cs_ = slice(oc * 128, (oc + 1) * 128)
                ps = PS.nxt()
                for kc in range(KC):
                    S.mm(ps[:, 0:n], wo[:, kc, cs_], m_[:, kc, 0:n], start=(kc == 0), stop=(kc == KC - 1))
                S.tt('dve', x[:, oc, 0:n], x[:, oc, 0:n], ps[:, 0:n], ALU.add)
            S.dma('pool', xTv[:, :, t0:t0 + n], x[:, :, 0:n])
        S.flush()


def ffn_ple(nc, S, E, l):
    T, TT, L = E['T'], E['TT'], E['L']
    PS = E['PS']
    ident, onesb = E['ident'], E['onesb']
    xT = E['xT']
    NT = 256
    tiles = [(i * NT, NT) for i in range(T // NT)] + [(T, TS)]
    xTv = xT
    with ExitStack() as ph:
        sb = lambda n, s, d=F32: ph.enter_context(nc.sbuf_tensor(f"{n}_L{l}", s, d))
        stage = Rot(nc, ph, "ff_st", [128, FF], F32, 2)
        w1 = sb("ff_w1", [128, KC, FF], BF)
        w3 = sb("ff_w3", [128, KC, FF], BF)
        w2 = sb("ff_w2", [128, FC, D], BF)
        load_cast(S, stage, w1, E['ffn_w1'][l], KC)
        load_cast(S, stage, w3, E['ffn_w3'][l], KC)
        load_cast(S, stage, w2, E['ffn_w2'][l], FC)
        gv = sb("ff_g", [128, 3 * KC])
        S.dma('sp', gv[:], E['vec8'][l])
        xp = Rot(nc, ph, "ff_x", [128, KC, NT], F32, 2)
        hp = Rot(nc, ph, "ff_h", [128, KC, NT], BF, 2)
        sqp = Rot(nc, ph, "ff_sq", [128, KC, NT], BF, 1)
        rsp = Rot(nc, ph, "ff_rs", [128, NT], F32, 2)
        hid = Rot(nc, ph, "ff_hid", [128, FC, NT], BF, 1)
        sil = Rot(nc, ph, "ff_sil", [128, NT], F32, 3)
        for (t0, n) in tiles:
            S.maybe_flush()
            x = xp.nxt()
            S.dma('sp', x[:, :, 0:n], xTv[:, :, t0:t0 + n])
            h = hp.nxt()
            rmsnorm_tile(S, PS, x, h, gv[:, KC:2 * KC], sqp, rsp, onesb, n)
            hd = hid.nxt()
            for fc in range(FC):
                cs_ = slice(fc * 128, (fc + 1) * 128)
                pa, pb_ = PS.nxt(), PS.nxt()
                for kc in range(KC):
                    S.mm(pa[:, 0:n], w1[:, kc, cs_], h[:, kc, 0:n], start=(kc == 0), stop=(kc == KC - 1))
                for kc in range(KC):
                    S.mm(pb_[:, 0:n], w3[:, kc, cs_], h[:, kc, 0:n], start=(kc == 0), stop=(kc == KC - 1))
                sl_ = sil.nxt()
                S.act(sl_[:, 0:n], pa[:, 0:n], AF.Silu)
                S.tt('dve', hd[:, fc, 0:n], sl_[:, 0:n], pb_[:, 0:n], ALU.mult)
            for oc in range(KC):
                cs_ = slice(oc * 128, (oc + 1) * 128)
                ps = PS.nxt()
                for fc in range(FC):
                    S.mm(ps[:, 0:n], w2[:, fc, cs_], hd[:, fc, 0:n], start=(fc == 0), stop=(fc == FC - 1))
                S.tt('dve', x[:, oc, 0:n], x[:, oc, 0:n], ps[:, 0:n], ALU.add)
            S.dma('pool', xTv[:, :, t0:t0 + n], x[:, :, 0:n])
        S.flush()
    NT = 512
    tiles = [(i * NT, NT) for i in range(T // NT)] + [(T, TS)]
    with ExitStack() as ph:
        sb = lambda n, s, d=F32: ph.enter_context(nc.sbuf_tensor(f"{n}_L{l}", s, d))
        stage = Rot(nc, ph, "pl_st", [128, D], F32, 3)
        wg = sb("pl_wg", [128, KC, D], BF)
        wp = sb("pl_wp", [128, 2, D], BF)
        load_cast(S, stage, wg, E['ple_gate'][l], KC)
        load_cast(S, stage, wp, E['ple_proj'][l], 2)
        gv = sb("pl_g", [128, 3 * KC])
        S.dma('sp', gv[:], E['vec8'][l])
        xp = Rot(nc, ph, "pl_x", [128, KC, NT], F32, 2)
        hp = Rot(nc, ph, "pl_h", [128, KC, NT], BF, 2)
        sqp = Rot(nc, ph, "pl_sq", [128, KC, NT], BF, 1)
        rsp = Rot(nc, ph, "pl_rs", [128, NT], F32, 2)
        pin = Rot(nc, ph, "pl_pin", [128, 4, PLE], F32, 2)
        pT = Rot(nc, ph, "pl_pT", [128, 2, NT], BF, 2)
        gt = Rot(nc, ph, "pl_gt", [128, NT], F32, 3)
        for (t0, n) in tiles:
            S.maybe_flush()
            x = xp.nxt()
            S.dma('sp', x[:, :, 0:n], xTv[:, :, t0:t0 + n])
            nb = (n + 127) // 128
            pb = min(n, 128)
            pi = pin.nxt()
            S.dma('act', pi[0:pb, 0:nb, :], E['p_in'][l, t0:t0 + n, :].rearrange("(b p) c -> p b c", p=pb))
            pt_ = pT.nxt()
            for b in range(nb):
                ps = PS.nxt()
                for c in range(2):
                    S.tr(ps[:, c * 128:c * 128 + pb], pi[0:pb, b, c * 128:(c + 1) * 128], ident[0:pb, 0:pb])
                S.cp('act', pt_[:, :, b * 128:b * 128 + pb], ps[:, 0:256].rearrange("p (c t) -> p c t", c=2)[:, :, 0:pb])
            h2 = hp.nxt()
            rmsnorm_tile(S, PS, x, h2, gv[:, 2 * KC:3 * KC], sqp, rsp, onesb, n)
            for oc in range(KC):
                cs_ = slice(oc * 128, (oc + 1) * 128)
                ps = PS.nxt()
                for kc in range(KC):
                    S.mm(ps[:, 0:n], wg[:, kc, cs_], h2[:, kc, 0:n], start=(kc == 0), stop=(kc == KC - 1))
                g_ = gt.nxt()
                S.act(g_[:, 0:n], ps[:, 0:n], AF.Sigmoid)
                ps = PS.nxt()
                for c in range(2):
                    S.mm(ps[:, 0:n], wp[:, c, cs_], pt_[:, c, 0:n], start=(c == 0), stop=(c == 1))
                S.tt('dve', g_[:, 0:n], g_[:, 0:n], ps[:, 0:n], ALU.mult)
                S.tt('pool', x[:, oc, 0:n], x[:, oc, 0:n], g_[:, 0:n], ALU.add)
            S.dma('pool', xTv[:, :, t0:t0 + n], x[:, :, 0:n])
        S.flush()


def final_out(nc, S, E):
    T = E['T']
    PS = E['PS']
    ident, onesb = E['ident'], E['onesb']
    xT, y_out = E['xT'], E['y_out']
    NT = 512
    tiles = [(i * NT, NT) for i in range(T // NT)] + [(T, TS)]
    xTv = xT
    with ExitStack() as ph:
        gf = ph.enter_context(nc.sbuf_tensor("fo_gf", [128, KC], F32))
        S.dma('sp', gf[:], E['vecf'])
        xp = Rot(nc, ph, "fo_x", [128, KC, NT], F32, 2)
        hp = Rot(nc, ph, "fo_h", [128, KC, NT], F32, 2)
        sqp = Rot(nc, ph, "fo_sq", [128, KC, NT], BF, 1)
        rsp = Rot(nc, ph, "fo_rs", [128, NT], F32, 2)
        yt = Rot(nc, ph, "fo_yt", [128, D], F32, 2)
        for (t0, n) in tiles:
            S.maybe_flush()
            x = xp.nxt()
            S.dma('sp', x[:, :, 0:n], xTv[:, :, t0:t0 + n])
            y_ = hp.nxt()
            rmsnorm_tile(S, PS, x, y_, gf, sqp, rsp, onesb, n)
            nb = (n + 127) // 128
            pb = min(n, 128)
            for b in range(nb):
                yt_ = yt.nxt()
                for half in range(2):
                    ps = PS.nxt()
                    for q in range(4):
                        kc = half * 4 + q
                        S.tr(ps[0:pb, q * 128:(q + 1) * 128], y_[:, kc, b * 128:b * 128 + pb], ident[:])
                    S.cp('act' if half == 0 else 'dve', yt_[0:pb, half * 512:(half + 1) * 512], ps[0:pb, :])
                S.dma('pool', y_out[t0 + b * 128:t0 + b * 128 + pb, :], yt_[0:pb, :])
        S.flush()


def host_consts(T):
    TT = T + TS
    half = 32
    inv = (10000.0 ** (-2.0 * np.arange(half, dtype=np.float32) / 64)).astype(np.float32)
    pos = np.concatenate([np.arange(T, dtype=np.float32)] + [8192.0 + np.arange(DS, dtype=np.float32)] * NSQ)
    ang = pos[:, None].astype(np.float32) * inv[None, :]
    cos = np.tile(np.cos(ang).astype(np.float32), (1, 12))
    sin = np.tile(np.sin(ang).astype(np.float32), (1, 12))
    ident = np.eye(128, dtype=np.float32)
    blk = np.zeros((128, 128), np.float32)
    blk[:64, :64] = 1
    blk[64:, 64:] = 1
    r = np.arange(128)[:, None]
    c = np.arange(128)[None, :]
    masks = np.stack([(c > r), (c < r), (c >= r), (c <= r)], axis=1).astype(np.float32)
    invcnt = np.zeros((128, 2, 2, 512), np.float32)
    for g4, w in enumerate((2, 4, 8, 16)):
        cidx, p0 = g4 // 2, (g4 % 2) * 64
        t = np.arange(512)
        invcnt[p0:p0 + 64, 0, cidx, :] = 1.0 / np.minimum(t + 1, w)
        invcnt[p0:p0 + 64, 1, cidx, :] = 1.0 / w
    smask = np.zeros((128, 21, 8), np.float32)
    smask_new = np.zeros((8, 3, 8), np.float32)
    bi = 0
    for gi, (w, dil) in enumerate(GROUPS):
        for bb in range(w // 128):
            i = bb * 128 + np.arange(128)[:, None]
            t = np.arange(8)[None, :]
            dist = w + t - i
            smask[:, bi, :] = ((dist >= 0) & (dist <= 128 * dil) & (dist % dil == 0)).astype(np.float32)
            bi += 1
        j = np.arange(8)[:, None]
        t = np.arange(8)[None, :]
        dist = t - j
        smask_new[:, gi, :] = ((dist >= 0) & (dist % dil == 0)).astype(np.float32)
    reset = np.ones((128, 2, 512), np.float32)
    reset[:, 0, ::128] = 0
    reset[:, 1, ::8] = 0
    return dict(c_cos=cos, c_sin=sin, c_ident=ident, c_blk=blk, c_masks=masks, c_invcnt=invcnt,
                c_smask=smask, c_smask_new=smask_new, c_reset=reset)


def pcol(v, n):
    v = np.asarray(v, np.float32)
    return np.ascontiguousarray(np.swapaxes(v.reshape(v.shape[:-1] + (n, 128)), -1, -2))


def make_in_maps(inp, T, L, n_cores=8, l0=0, xT_prev=None):
    PER_LAYER = {'state_pool', 'state_shift', 'state_wkv', 'cache_kv_w128', 'cache_kv_w512', 'cache_kv_w2048', 'p_prompt',
                 'p_sample', 'norm_mix_g', 'w_in', 'pool_w_grp', 'pool_scale', 'rwkv_mu', 'rwkv_w0', 'rwkv_w2', 'rwkv_a0',
                 'rwkv_a2', 'rwkv_g2', 'rwkv_k_k', 'rwkv_k_a', 'rwkv_r_k', 'rwkv_ln_g', 'rwkv_ln_b', 'proj_pool', 'proj_attn',
                 'proj_rwkv', 'w_out', 'norm_ffn_g', 'ffn_w1', 'ffn_w3', 'ffn_w2', 'norm_ple_g', 'ple_proj', 'ple_gate'}
    f = lambda k: (np.asarray(inp[k], np.float32)[l0:l0 + L] if k in PER_LAYER else np.asarray(inp[k], np.float32))
    consts = host_consts(T)
    wblk = np.zeros((L, 2, 128, 128), np.float32)
    pw = f('pool_w_grp')
    for g4 in range(4):
        c, p0 = g4 // 2, (g4 % 2) * 64
        wblk[:, c, p0:p0 + 64, p0:p0 + 64] = pw[:, g4]
    vec8 = np.concatenate([pcol(f('norm_mix_g'), 8), pcol(f('norm_ffn_g'), 8), pcol(f('norm_ple_g'), 8)], axis=-1)
    vec3 = np.concatenate([pcol(f('rwkv_w0'), 3), pcol(f('rwkv_a0'), 3), pcol(f('rwkv_k_k'), 3), pcol(f('rwkv_k_a'), 3),
                           pcol(f('rwkv_r_k').reshape(L, 384), 3)], axis=-1)
    shared = dict(
        w_in=f('w_in'), pool_wblk=wblk, vec8=np.ascontiguousarray(vec8), vecf=pcol(f('norm_final_g'), 8),
        pool_scale=pcol(f('pool_scale'), 2), rw_mu=pcol(f('rwkv_mu'), 11), rw_vec3=np.ascontiguousarray(vec3),
        rw_w2a2=np.ascontiguousarray(np.concatenate([f('rwkv_w2'), f('rwkv_a2')], axis=1)), rw_g2=f('rwkv_g2'),
        rw_ln=np.ascontiguousarray(np.stack([f('rwkv_ln_g'), f('rwkv_ln_b')], axis=1)),
        proj_pool=f('proj_pool'), proj_attn=f('proj_attn'), proj_rwkv=f('proj_rwkv'), w_out=f('w_out'),
        ffn_w1=f('ffn_w1'), ffn_w3=f('ffn_w3'), ffn_w2=f('ffn_w2'), ple_proj=f('ple_proj'), ple_gate=f('ple_gate'),
        **consts)
    maps = []
    xp, xs, pp, ps_ = f('x_prompt'), f('x_sample'), f('p_prompt'), f('p_sample')
    for c in range(n_cores):
        b = c % xp.shape[0]
        sq = slice(c * NSQ, (c + 1) * NSQ)
        m = dict(shared)
        if xT_prev is None:
            m['x_in'] = np.ascontiguousarray(np.concatenate([xp[b, :T], xs[sq].reshape(TS, D)], axis=0))
        else:
            m['xT_in'] = np.ascontiguousarray(xT_prev[c])
        m['p_in'] = np.ascontiguousarray(np.concatenate([pp[:L, b, :T], ps_[:L, sq].reshape(L, TS, PLE)], axis=1))
        m['st_pool'] = np.ascontiguousarray(f('state_pool')[:L, sq])
        m['st_shift'] = np.ascontiguousarray(f('state_shift')[:L, sq])
        m['st_wkv'] = np.ascontiguousarray(f('state_wkv')[:L, sq])
        for (w, _), k in zip(GROUPS, ('cache_kv_w128', 'cache_kv_w512', 'cache_kv_w2048')):
            m[f'cache{w}'] = np.ascontiguousarray(f(k)[:L, sq].reshape(L, NSQ, w, 256))
        maps.append(m)
    return maps


_NC_CACHE = {}


def run(inp, T, L, debug=False, first=True, last=True, l0=0, xT_prev=None):
    key = (T, L, debug, first, last)
    if key not in _NC_CACHE:
        _NC_CACHE[key] = build(T, L, debug, first, last)
    nc = _NC_CACHE[key]
    maps = make_in_maps(inp, T, L, l0=l0, xT_prev=xT_prev)
    res = run_bass_kernel_spmd(nc, maps, core_ids=list(range(8)))
    return res.results


def assemble(R, T, L, NB):
    y_p = np.stack([R[b]['y_out'][:T] for b in range(NB)])
    y_s = np.concatenate([R[c]['y_out'][T:].reshape(NSQ, DS, D) for c in range(8)])
    def ps(name, tail):
        p = np.stack([R[b][name][:, 0] for b in range(NB)], axis=1)
        s = np.concatenate([R[c][name][:, 1:] for c in range(8)], axis=1)
        return p.reshape((L, NB) + tail), s.reshape((L, 8 * NSQ) + tail)
    pool_p, pool_s = ps('pool_o', (15, 256))
    sh_p, sh_s = ps('shift_o', (RW,))
    wk_p, wk_s = ps('wkv_o', (6, 64, 64))
    outs = [y_p, y_s, pool_p, pool_s, sh_p, sh_s, wk_p, wk_s]
    for w, _ in GROUPS:
        a, b = ps(f'kv{w}_o', (w, 2, 2, 64))
        outs += [a, b]
    return tuple(np.ascontiguousarray(o, dtype=np.float32) for o in outs)


def kernel(**inputs):
    T, L = 8192, 4
    R1 = run(inputs, T, 2, first=True, last=False, l0=0)
    xT_prev = [np.asarray(R1[c]['xT']) for c in range(8)]
    R2 = run(inputs, T, 2, first=False, last=True, l0=2, xT_prev=xT_prev)
    R = []
    for c in range(8):
        d = {'y_out': R2[c]['y_out']}
        for k in ('pool_o', 'shift_o', 'wkv_o', 'kv128_o', 'kv512_o', 'kv2048_o'):
            d[k] = np.concatenate([R1[c][k], R2[c][k]], axis=0)
        R.append(d)
    return assemble(R, T, L, 2)
```

```python
import math
from contextlib import ExitStack
import numpy as np
import concourse.bass as bass
import concourse.mybir as mybir
from concourse.bass_utils import run_bass_kernel_spmd

F32 = mybir.dt.float32
BF = mybir.dt.bfloat16
AF = mybir.ActivationFunctionType
ALU = mybir.AluOpType
AX = mybir.AxisListType

D = 1024
KC = 8
NCOL = 5888
FF = 2816
FC = 22
PLE = 256
NSQ = 4
DS = 8
TS = NSQ * DS
RW = 1408
GROUPS = ((128, 1), (512, 4), (2048, 16))
C0 = math.exp(-0.5)
ENGS = ['pe', 'act', 'dve', 'pool', 'sp']
BLK = {'pe': 'tensor', 'act': 'scalar', 'dve': 'vector', 'pool': 'gpsimd', 'sp': 'sync'}


class Sched:
    def __init__(self, nc, stack):
        self.nc = nc
        self.stack = stack
        self.ops = []
        self.cnt = {}
        self.dcnt = {}
        self.W = {}
        self.R = {}
        self.waited = {e: {} for e in ENGS}
        self.sems = {}

    def sem(self, lane, idx):
        dma = lane.startswith('dma')
        eps = 1900 if dma else 16000
        ep = (idx - 1) // eps
        key = (lane, ep)
        if key not in self.sems:
            self.sems[key] = self.stack.enter_context(self.nc.semaphore(f"s_{lane}_{ep}"))
        return self.sems[key], ((idx - 1) % eps + 1) * (16 if dma else 1)

    NSLOT = 8
    limit = None
    nflush = 0

    def add(self, eng, fn, reads, writes, dma=False):
        if self.limit is not None and self.nflush >= self.limit:
            return
        deps = {}
        if dma:
            k = self.dcnt[eng] = self.dcnt.get(eng, 0) + 1
            lane = f"dma_{eng}_{(k - 1) % self.NSLOT}"
            idx = self.cnt[lane] = self.cnt.get(lane, 0) + 1
            if idx > 1:
                deps[lane] = idx - 1
        else:
            lane = eng
            idx = self.cnt[lane] = self.cnt.get(lane, 0) + 1
        for k in reads:
            for l, i in self.W.get(k, {}).items():
                if l == lane and eng == 'pe' and not dma:
                    continue
                deps[l] = max(deps.get(l, 0), i)
        for k in writes:
            for l, i in self.W.get(k, {}).items():
                if l == lane and not dma:
                    continue
                deps[l] = max(deps.get(l, 0), i)
            for l, i in self.R.get(k, {}).items():
                if l == lane and not dma:
                    continue
                deps[l] = max(deps.get(l, 0), i)
        waits = []
        wd = self.waited[eng]
        for l, i in deps.items():
            if wd.get(l, 0) >= i:
                continue
            wd[l] = i
            waits.append(self.sem(l, i))
        for k in reads:
            self.R.setdefault(k, {})[lane] = idx
        for k in writes:
            self.W[k] = {lane: idx}
            self.R[k] = {}
        self.ops.append((eng, fn, waits, self.sem(lane, idx), 16 if dma else 1))

    def barrier(self):
        for eng in ENGS:
            waits = []
            wd = self.waited[eng]
            for l, i in self.cnt.items():
                if wd.get(l, 0) >= i:
                    continue
                wd[l] = i
                waits.append(self.sem(l, i))
            if waits:
                self.ops.append((eng, None, waits, None, 0))
        self.W = {}
        self.R = {}

    def maybe_flush(self, max_ops=2500, max_dma=40):
        nd = {}
        for o in self.ops:
            if o[4] == 16:
                nd[o[0]] = nd.get(o[0], 0) + 1
        if len(self.ops) > max_ops or (nd and max(nd.values()) > max_dma):
            self.nflush -= 1
            self.flush()

    def flush(self):
        self.nflush += 1
        self.barrier()
        ops = self.ops
        self.ops = []
        if not ops:
            return
        with self.nc.Block() as block:
            for eng in ENGS:
                mine = [o for o in ops if o[0] == eng]
                if not mine:
                    continue

                def body(e, mine=mine):
                    for (_, fn, waits, sv, inc) in mine:
                        if fn is None:
                            for (ws, wv) in waits:
                                e.wait_ge(ws, wv)
                            continue
                        for (ws, wv) in waits[:-1]:
                            e.wait_ge(ws, wv)
                        ins = fn(e)
                        if waits:
                            ins._wait_ge(waits[-1][0], waits[-1][1])
                        ins.then_inc(sv[0], inc)
                getattr(block, BLK[eng])(body)

    @staticmethod
    def nm(*aps):
        return [a.tensor.name for a in aps if a is not None and hasattr(a, 'tensor')]

    def mm(self, out, lhsT, rhs, start=True, stop=True):
        self.add('pe', lambda e: e.matmul(out, lhsT, rhs, start=start, stop=stop), self.nm(lhsT, rhs), self.nm(out))

    def tr(self, out, in_, ident):
        self.add('pe', lambda e: e.transpose(out, in_, ident), self.nm(in_, ident), self.nm(out))

    def act(self, out, in_, func, bias=None, scale=1.0):
        kw = {}
        if bias is not None:
            kw['bias'] = bias
        self.add('act', lambda e: e.activation(out, in_, func, scale=scale, **kw), self.nm(in_, bias, scale), self.nm(out))

    def cp(self, eng, out, in_):
        if eng == 'act':
            self.act(out, in_, AF.Copy)
        else:
            self.add(eng, lambda e: e.tensor_copy(out, in_), self.nm(in_), self.nm(out))

    def tt(self, eng, out, in0, in1, op):
        self.add(eng, lambda e: e.tensor_tensor(out, in0, in1, op), self.nm(in0, in1), self.nm(out))

    def ts(self, eng, out, in0, s1, s2, op0, op1=None):
        if op1 is None:
            self.add(eng, lambda e: e.tensor_scalar(out, in0, s1, None, op0), self.nm(in0, s1), self.nm(out))
        else:
            self.add(eng, lambda e: e.tensor_scalar(out, in0, s1, s2, op0, op1), self.nm(in0, s1, s2), self.nm(out))

    def stt(self, eng, out, in0, sc, in1, op0, op1):
        eng = 'dve'
        self.add(eng, lambda e: e.scalar_tensor_tensor(out, in0, sc, in1, op0, op1), self.nm(in0, sc, in1), self.nm(out))

    def ms(self, eng, ap, val):
        self.add(eng, lambda e: e.memset(ap, val), [], self.nm(ap))

    def red(self, eng, out, in_, op):
        self.add(eng, lambda e: e.tensor_reduce(out, in_, AX.X, op), self.nm(in_), self.nm(out))

    def scan(self, out, d0, d1, init, op0, op1):
        self.add('dve', lambda e: e.tensor_tensor_scan(out, d0, d1, init, op0, op1), self.nm(d0, d1), self.nm(out))

    def dma(self, eng, out, in_, slow=False):
        if slow:
            fn = lambda e: e.dma_start(out=out, in_=in_, allow_slow_non_contiguous=True)
        else:
            fn = lambda e: e.dma_start(out=out, in_=in_)
        self.add(eng, fn, self.nm(in_), self.nm(out), dma=True)


class Rot:
    uid = 0

    def __init__(self, nc, stack, name, shape, dtype, n, psum=False):
        mk = nc.psum_tensor if psum else nc.sbuf_tensor
        Rot.uid += 1
        self.t = [stack.enter_context(mk(f"{name}{i}_u{Rot.uid}", shape, dtype)) for i in range(n)]
        self.i = 0

    def nxt(self):
        t = self.t[self.i % len(self.t)]
        self.i += 1
        return t


def build(T, L, debug=False, first=True, last=True):
    TT = T + TS
    NTP = T // 512
    nc = bass.Bass("TRN2", target_bir_lowering=False)
    dt_in = lambda n, s, d=F32: nc.dram_tensor(n, list(s), d, kind="ExternalInput").ap()
    dt_out = lambda n, s, d=F32: nc.dram_tensor(n, list(s), d, kind="ExternalOutput").ap()
    dt_scr = lambda n, s, d=F32: nc.dram_tensor(n, list(s), d, kind="ExternalOutput" if debug else "Internal").ap()

    if first:
        x_in = dt_in("x_in", [TT, D])
    else:
        xT_in = dt_in("xT_in", [128, KC, TT])
    p_in = dt_in("p_in", [L, TT, PLE])
    st_pool = dt_in("st_pool", [L, NSQ, 15, 256])
    st_shift = dt_in("st_shift", [L, NSQ, RW])
    st_wkv = dt_in("st_wkv", [L, NSQ, 6, 64, 64])
    cache = [dt_in(f"cache{w}", [L, NSQ, w, 256]) for w, _ in GROUPS]
    w_in = dt_in("w_in", [L, D, NCOL])
    pool_wblk = dt_in("pool_wblk", [L, 2, 128, 128])
    vec8 = dt_in("vec8", [L, 128, 3 * KC])
    vecf = dt_in("vecf", [128, KC])
    pool_scale = dt_in("pool_scale", [L, 128, 2])
    rw_mu = dt_in("rw_mu", [L, 128, 11])
    rw_vec3 = dt_in("rw_vec3", [L, 128, 5 * 3])
    rw_w2a2 = dt_in("rw_w2a2", [L, 128, 384])
    rw_g2 = dt_in("rw_g2", [L, 128, 384])
    rw_ln = dt_in("rw_ln", [L, 2, 384])
    proj_pool = dt_in("proj_pool", [L, 256, D])
    proj_attn = dt_in("proj_attn", [L, 128, D])
    proj_rwkv = dt_in("proj_rwkv", [L, 384, D])
    w_out = dt_in("w_out", [L, D, D])
    ffn_w1 = dt_in("ffn_w1", [L, D, FF])
    ffn_w3 = dt_in("ffn_w3", [L, D, FF])
    ffn_w2 = dt_in("ffn_w2", [L, FF, D])
    ple_proj = dt_in("ple_proj", [L, PLE, D])
    ple_gate = dt_in("ple_gate", [L, D, D])
    c_cos = dt_in("c_cos", [TT, 384])
    c_sin = dt_in("c_sin", [TT, 384])
    c_ident = dt_in("c_ident", [128, 128])
    c_blk = dt_in("c_blk", [128, 128])
    c_masks = dt_in("c_masks", [128, 4, 128])
    c_invcnt = dt_in("c_invcnt", [128, 2, 2, 512])
    c_smask = dt_in("c_smask", [128, 21, 8])
    c_smask_new = dt_in("c_smask_new", [8, 3, 8])
    c_reset = dt_in("c_reset", [128, 2, 512])

    y_out = dt_out("y_out", [TT, D]) if last else None
    pool_o = dt_out("pool_o", [L, 1 + NSQ, 15, 256])
    shift_o = dt_out("shift_o", [L, 1 + NSQ, RW])
    wkv_o = dt_out("wkv_o", [L, 1 + NSQ, 6, 64, 64])
    kv_o = [dt_out(f"kv{w}_o", [L, 1 + NSQ, w, 256]) for w, _ in GROUPS]

    xT = dt_scr("xT", [128, KC, TT]) if last else dt_out("xT", [128, KC, TT])
    zpT = dt_scr("zpT", [128, 2, TT])
    zrT = dt_scr("zrT", [128, 11, TT])
    gT = dt_scr("gT", [128, 24, TT], BF)
    qk_tok = dt_scr("qk_tok", [TT, 768], BF)
    v_tok = dt_scr("v_tok", [TT, 6, 65], BF)
    og = [dt_scr(f"og{g}", [T, 2, 65]) for g in range(3)]
    ypT = dt_scr("ypT", [128, 2, TT], BF)
    yaT = dt_scr("yaT", [128, TT], BF)
    yrT = dt_scr("yrT", [128, 3, TT], BF)

    top = ExitStack()
    S = Sched(nc, top)
    with top:
        ident = top.enter_context(nc.sbuf_tensor("ident", [128, 128], F32))
        identb = top.enter_context(nc.sbuf_tensor("identb", [128, 128], BF))
        blkones = top.enter_context(nc.sbuf_tensor("blkones", [128, 128], BF))
        onesb = top.enter_context(nc.sbuf_tensor("onesb", [128, 128], BF))
        masks = top.enter_context(nc.sbuf_tensor("masks", [128, 4, 128], F32))
        masksb = top.enter_context(nc.sbuf_tensor("masksb", [128, 4, 128], BF))
        blkf = top.enter_context(nc.sbuf_tensor("blkf", [128, 128], F32))
        S.dma('sp', ident[:], c_ident)
        S.dma('sp', blkf[:], c_blk)
        S.dma('sp', masks[:], c_masks)
        S.cp('dve', identb[:], ident[:])
        S.cp('dve', blkones[:], blkf[:])
        S.cp('dve', masksb[:], masks[:])
        S.ms('dve', onesb[:], 1.0)

        PS = Rot(nc, top, "ps", [128, 512], F32, 5, psum=True)
        PA = top.enter_context(nc.psum_tensor("psacc", [128, 512], F32))
        PSB = Rot(nc, top, "psb", [128, 1024], BF, 2, psum=True)

        with ExitStack() as ph:
            xin = Rot(nc, ph, "p0x", [128, D], F32, 2)
            xst = Rot(nc, ph, "p0s", [128, KC, 128], F32, 2)
            blocks = [(i * 128, 128) for i in range(T // 128)] + [(T, TS)]
            if not first:
                blocks = []
                for kc in range(KC):
                    S.dma('sp' if kc % 2 == 0 else 'act', xT[:, kc, :], xT_in[:, kc, :])
            for (t0, n) in blocks:
                xi = xin.nxt()
                S.dma('sp', xi[0:n, :], x_in[t0:t0 + n, :])
                st = xst.nxt()
                for half in range(2):
                    ps = PS.nxt()
                    for q in range(4):
                        kc = half * 4 + q
                        S.tr(ps[:, q * 128:q * 128 + n], xi[0:n, kc * 128:(kc + 1) * 128], ident[0:n, 0:n])
                    src = ps[:].rearrange("p (q t) -> p q t", q=4)[:, :, 0:n]
                    S.cp('act' if half == 0 else 'dve', st[:, half * 4:half * 4 + 4, 0:n], src)
                S.dma('pool', xT[:, :, t0:t0 + n], st[:, :, 0:n])
                S.maybe_flush()
            for gi, (w, dil) in enumerate(GROUPS):
                for l in range(L):
                    S.dma('sp', kv_o[gi][l, 1:1 + NSQ, 0:w - DS, :], cache[gi][l, :, DS:w, :])
            for l in range(L):
                S.dma('sp', pool_o[l, 1:1 + NSQ, 0:7, :], st_pool[l, :, 8:15, :])
            S.flush()

        for l in range(L):
            layer(nc, S, locals(), l)
        if last:
            final_out(nc, S, locals())
    return nc


def rmsnorm_tile(S, PS, x, h, g, sqp, rsp, onesb, n):
    sq = sqp.nxt()
    S.act(sq[:, :, 0:n], x[:, :, 0:n], AF.Square)
    ps = PS.nxt()
    for kc in range(KC):
        S.mm(ps[:, 0:n], onesb[:], sq[:, kc, 0:n], start=(kc == 0), stop=(kc == KC - 1))
    rs = rsp.nxt()
    S.ts('dve', rs[:, 0:n], ps[:, 0:n], 1.0 / D, 1e-6, ALU.mult, ALU.add)
    S.act(rs[:, 0:n], rs[:, 0:n], AF.Ln)
    S.act(rs[:, 0:n], rs[:, 0:n], AF.Exp, scale=-0.5)
    for kc in range(KC):
        S.stt('dve' if kc % 2 == 0 else 'pool', h[:, kc, 0:n], x[:, kc, 0:n], g[:, kc:kc + 1], rs[:, 0:n], ALU.mult, ALU.mult)
    return rs


def load_cast(S, stage, dst, src, n_rows_chunks, engs=('dve', 'pool', 'act')):
    for c in range(n_rows_chunks):
        st = stage.nxt()
        n = src.shape[1]
        S.dma('sp' if c % 2 == 0 else 'act', st[:, 0:n], src[c * 128:(c + 1) * 128, :])
        S.cp(engs[c % len(engs)], dst[:, c, :], st[:, 0:n])


def layer(nc, S, E, l):
    g = E.get
    T, TT, L = E['T'], E['TT'], E['L']
    NTP = E['NTP']
    PS, PSB = E['PS'], E['PSB']
    ident, identb, blkones, onesb, masks, masksb = E['ident'], E['identb'], E['blkones'], E['onesb'], E['masks'], E['masksb']
    xT, zpT, zrT, gT, qk_tok, v_tok, og, ypT, yaT, yrT = (E[k] for k in ('xT', 'zpT', 'zrT', 'gT', 'qk_tok', 'v_tok', 'og', 'ypT', 'yaT', 'yrT'))
    kv_o, pool_o, shift_o, wkv_o, y_out = E['kv_o'], E['pool_o'], E['shift_o'], E['wkv_o'], E['y_out']
    tiles = [(i * 512, 512) for i in range(NTP)] + [(T, TS)]
    tiles1 = [(i * 256, 256) for i in range(T // 256)] + [(T, TS)]
    xTv = xT

    with ExitStack() as ph:
        sb = lambda n, s, d=F32: ph.enter_context(nc.sbuf_tensor(f"{n}_L{l}", s, d))
        wbf = sb("p1w", [128, KC, NCOL], BF)
        stage = Rot(nc, ph, "p1st", [128, NCOL // 2], F32, 2)
        for kc in range(KC):
            for hf in range(2):
                st = stage.nxt()
                c0 = hf * (NCOL // 2)
                S.dma('sp' if hf == 0 else 'act', st[:], E['w_in'][l, kc * 128:(kc + 1) * 128, c0:c0 + NCOL // 2])
                S.cp(('dve', 'pool')[(kc * 2 + hf) % 2], wbf[:, kc, c0:c0 + NCOL // 2], st[:])
        gv = sb("p1g", [128, 3 * KC])
        S.dma('sp', gv[:], E['vec8'][l])
        xp = Rot(nc, ph, "p1x", [128, KC, 256], F32, 2)
        hp = Rot(nc, ph, "p1h", [128, KC, 256], BF, 2)
        sqp = Rot(nc, ph, "p1sq", [128, KC, 256], BF, 1)
        rsp = Rot(nc, ph, "p1rs", [128, 256], F32, 2)
        stf = Rot(nc, ph, "p1stf", [128, 6, 256], F32, 2)
        stg = Rot(nc, ph, "p1stg", [128, 8, 256], BF, 2)
        cosp = Rot(nc, ph, "p1cos", [128, 2, 384], F32, 2)
        sinp = Rot(nc, ph, "p1sin", [128, 2, 384], F32, 2)
        tmp = Rot(nc, ph, "p1tmp", [128, 6, 32], F32, 4)
        qbf = Rot(nc, ph, "p1qb", [128, 768], BF, 2)
        kf = Rot(nc, ph, "p1kf", [128, 384], F32, 2)
        vf = Rot(nc, ph, "p1vf", [128, 384], F32, 2)
        vb = [sb(f"p1vb{i}", [128, 6, 65], BF) for i in range(2)]
        for t_ in vb:
            S.ms('pool', t_[:], 1.0)
        zpv = zpT
        zrv = zrT
        gTv = gT
        vbi = 0
        for (t0, n) in tiles1:
            S.maybe_flush()
            sample = (t0 == T)
            x = xp.nxt()
            S.dma('sp', x[:, :, 0:n], xTv[:, :, t0:t0 + n])
            cs, sn = cosp.nxt(), sinp.nxt()
            nb = (n + 127) // 128
            pb = min(n, 128)
            S.dma('act', cs[0:pb, 0:nb, :], E['c_cos'][t0:t0 + n, :].rearrange("(b p) c -> p b c", p=pb))
            S.dma('act', sn[0:pb, 0:nb, :], E['c_sin'][t0:t0 + n, :].rearrange("(b p) c -> p b c", p=pb))
            h = hp.nxt()
            rmsnorm_tile(S, PS, x, h, gv[:, 0:KC], sqp, rsp, onesb, n)

            def proj(c):
                ps = PS.nxt()
                for kc in range(KC):
                    S.mm(ps[:, 0:n], wbf[:, kc, c * 128:(c + 1) * 128], h[:, kc, 0:n], start=(kc == 0), stop=(kc == KC - 1))
                return ps
            st = stf.nxt()
            for c in range(2):
                ps = proj(c)
                S.cp('act', st[:, c, 0:n], ps[:, 0:n])
            S.dma('pool', zpv[:, :, t0:t0 + n], st[:, 0:2, 0:n])
            if sample:
                for s in range(NSQ):
                    for c in range(2):
                        S.dma('pool', pool_o[l, 1 + s, 7:15, c * 128:(c + 1) * 128].rearrange("t p -> p t"), st[:, c, s * DS:(s + 1) * DS], slow=True)
            elif t0 + n == T:
                for c in range(2):
                    S.dma('pool', pool_o[l, 0, :, c * 128:(c + 1) * 128].rearrange("t p -> p t"), st[:, c, n - 15:n], slow=True)
            for part, cl in enumerate(((0, 6), (6, 11))):
                st = stf.nxt()
                for j in range(cl[0], cl[1]):
                    ps = proj(11 + j)
                    S.cp('act' if j % 2 == 0 else 'dve', st[:, j - cl[0], 0:n], ps[:, 0:n])
                S.dma('pool', zrv[:, cl[0]:cl[1], t0:t0 + n], st[:, 0:cl[1] - cl[0], 0:n])
                if sample:
                    for s in range(NSQ):
                        S.dma('pool', shift_o[l, 1 + s, cl[0] * 128:cl[1] * 128].rearrange("(c p) -> p c", p=128),
                              st[:, 0:cl[1] - cl[0], s * DS + DS - 1], slow=True)
                elif t0 + n == T:
                    S.dma('pool', shift_o[l, 0, cl[0] * 128:cl[1] * 128].rearrange("(c p) -> p c", p=128),
                          st[:, 0:cl[1] - cl[0], n - 1], slow=True)
            for gi in range(3):
                sg_ = stg.nxt()
                for j in range(8):
                    ps = proj(22 + gi * 8 + j)
                    S.act(sg_[:, j, 0:n], ps[:, 0:n], AF.Sigmoid)
                S.dma('pool', gTv[:, gi * 8:(gi + 1) * 8, t0:t0 + n], sg_[:, :, 0:n])
            for tb in range(nb):
                m = min(128, n - tb * 128)
                tok0 = t0 + tb * 128
                pss = []
                for ci in range(3):
                    ps = PS.nxt()
                    c0 = 256 + ci * 384
                    for kc in range(KC):
                        S.mm(ps[0:m, 0:384], h[:, kc, tb * 128:tb * 128 + m], wbf[:, kc, c0:c0 + 384], start=(kc == 0), stop=(kc == KC - 1))
                    pss.append(ps)
                qb = qbf.nxt()
                kf_ = kf.nxt()
                for qi in range(2):
                    src = pss[qi][0:m, 0:384].rearrange("p (h two c) -> p h two c", two=2, c=32)
                    x1, x2 = src[:, :, 0, :], src[:, :, 1, :]
                    cc = cs[0:m, tb, :].rearrange("p (h c) -> p h c", c=32)[:, 0:6, :] if False else cs[0:m, tb, 0:192].rearrange("p (h c) -> p h c", c=32)
                    ss_ = sn[0:m, tb, 0:192].rearrange("p (h c) -> p h c", c=32)
                    if qi == 0:
                        dst = qb[0:m, 0:384].rearrange("p (h two c) -> p h two c", two=2, c=32)
                    else:
                        dst = kf_[0:m, :].rearrange("p (h two c) -> p h two c", two=2, c=32)
                    e1, e2 = ('dve', 'dve')
                    t1, t2, t3, t4 = tmp.nxt(), tmp.nxt(), tmp.nxt(), tmp.nxt()
                    S.tt('dve', t1[0:m], x1, cc, ALU.mult)
                    S.tt('dve', t2[0:m], x2, ss_, ALU.mult)
                    S.tt('pool', dst[:, :, 0, :], t1[0:m], t2[0:m], ALU.subtract)
                    S.tt('dve', t3[0:m], x2, cc, ALU.mult)
                    S.tt('dve', t4[0:m], x1, ss_, ALU.mult)
                    S.tt('pool', dst[:, :, 1, :], t3[0:m], t4[0:m], ALU.add)
                S.cp('pool', qb[0:m, 384:768], kf_[0:m, :])
                S.dma('pool', qk_tok[tok0:tok0 + m, :], qb[0:m, :])
                vf_ = vf.nxt()
                S.cp('act', vf_[0:m, :], pss[2][0:m, 0:384])
                vb_ = vb[vbi % 2]
                vbi += 1
                S.cp('pool', vb_[0:m, :, 0:64], vf_[0:m, :].rearrange("p (h c) -> p h c", c=64))
                S.dma('pool', v_tok[tok0:tok0 + m, :, :], vb_[0:m, :, :])
                for gi, (w, dil) in enumerate(GROUPS):
                    kk_ = kf_[:, gi * 128:(gi + 1) * 128]
                    vv_ = vf_[:, gi * 128:(gi + 1) * 128]
                    if sample:
                        for s in range(NSQ):
                            dst = kv_o[gi][l, 1 + s, w - DS:w, :]
                            S.dma('pool', dst[:, 0:128], kk_[s * DS:(s + 1) * DS, :])
                            S.dma('pool', dst[:, 128:256], vv_[s * DS:(s + 1) * DS, :])
                    elif tok0 >= T - w:
                        r0 = tok0 - (T - w)
                        S.dma('pool', kv_o[gi][l, 0, r0:r0 + 128, 0:128], kk_[:, :])
                        S.dma('pool', kv_o[gi][l, 0, r0:r0 + 128, 128:256], vv_[:, :])
        S.flush()

    with ExitStack() as ph:
        sb = lambda n, s, d=F32: ph.enter_context(nc.sbuf_tensor(f"{n}_L{l}", s, d))
        wst = sb("pa_wst", [128, 2, 128])
        wblk = sb("pa_w", [128, 2, 128], BF)
        S.dma('sp', wst[:], E['pool_wblk'][l].rearrange("c p m -> p c m"))
        S.cp('dve', wblk[:], wst[:])
        scl = sb("pa_scl", [128, 2])
        S.dma('sp', scl[:], E['pool_scale'][l])
        inv = sb("pa_inv", [128, 2, 2, 512])
        S.dma('sp', inv[:], E['c_invcnt'])
        extp = Rot(nc, ph, "pa_ext", [128, 2, 527], F32, 2)
        s2p = Rot(nc, ph, "pa_s2", [128, 2, 527], F32, 1)
        s4p = Rot(nc, ph, "pa_s4", [128, 2, 527], F32, 1)
        s8p = Rot(nc, ph, "pa_s8", [128, 2, 527], F32, 1)
        s16p = Rot(nc, ph, "pa_s16", [128, 2, 527], F32, 1)
        mp = Rot(nc, ph, "pa_m", [128, 2, 512], F32, 1)
        dp = Rot(nc, ph, "pa_d", [128, 2, 512], BF, 2)
        yp = Rot(nc, ph, "pa_y", [128, 2, 512], BF, 2)
        zpv = zpT
        ypv = ypT

        def pool_core(ext, n, which, lead):
            sl = lambda a, b: (slice(None),) * (2 + lead) + (slice(a, b),)
            s2, s4, s8, s16 = s2p.nxt(), s4p.nxt(), s8p.nxt(), s16p.nxt()
            return s2, s4, s8, s16

        for ti, (t0, n) in enumerate(tiles):
            S.maybe_flush()
            sample = (t0 == T)
            ext = extp.nxt()
            if not sample:
                if ti == 0:
                    S.ms('pool', ext[:, :, 0:15], 0.0)
                    S.dma('sp', ext[:, :, 15:15 + n], zpv[:, :, 0:n])
                else:
                    S.dma('sp', ext[:, :, 0:15 + n], zpv[:, :, t0 - 15:t0 + n])
                E_ = ext[:, :, 0:15 + n]
                W_ = 15 + n
                shp = lambda a, lo, hi: a[:, :, lo:hi]
                nn = n
                M3 = lambda a: a[:, :, 0:n]
                invv = inv[:, 0 if ti == 0 else 1, :, 0:n]
            else:
                hst = sb("pa_hst", [15, NSQ, 256])
                S.dma('sp', hst[:], E['st_pool'][l].rearrange("s t c -> t s c"))
                e4 = ext[:, :, 0:NSQ * 23].rearrange("p c (s w) -> p c s w", w=23)
                for s in range(NSQ):
                    ps = PS.nxt()
                    for c in range(2):
                        S.tr(ps[:, c * 16:c * 16 + 15], hst[0:15, s, c * 128:(c + 1) * 128], ident[0:15, 0:15])
                    S.cp('dve', e4[:, :, s, 0:15], ps[:, 0:32].rearrange("p (c w) -> p c w", w=16)[:, :, 0:15])
                for c in range(2):
                    S.dma('sp', e4[:, c, :, 15:23], zpv[:, c, T:T + TS].rearrange("p (s w) -> p s w", w=DS))
                nn = DS
                shp = None
                invv = None
            s2, s4, s8, s16, m_, d_ = s2p.nxt(), s4p.nxt(), s8p.nxt(), s16p.nxt(), mp.nxt(), dp.nxt()
            if not sample:
                V = lambda a, lo, hi: a[:, :, lo:hi]
                O = lambda a: a[:, :, 0:n]
                IV = lambda c, p0: inv[p0:p0 + 64, 0 if ti == 0 else 1, c, 0:n]
                P = lambda a, c, p0, lo, hi: a[p0:p0 + 64, c, lo:hi]
                U = ext[:, :, 15:15 + n]
            else:
                v4 = lambda a: a[:, :, 0:NSQ * 23].rearrange("p c (s w) -> p c s w", w=23)
                V = lambda a, lo, hi: v4(a)[:, :, :, lo:hi]
                o4 = lambda a: a[:, :, 0:TS].rearrange("p c (s w) -> p c s w", w=DS)
                O = o4
                IV = lambda c, p0: inv[p0:p0 + 64, 1, c, 0:TS].rearrange("p (s w) -> p s w", w=DS)
                P = lambda a, c, p0, lo, hi: v4(a)[p0:p0 + 64, c, :, lo:hi]
                U = v4(ext)[:, :, :, 15:23]
            W_ = 15 + nn
            S.tt('dve', V(s2, 0, W_ - 1), V(ext, 0, W_ - 1), V(ext, 1, W_), ALU.add)
            S.tt('pool', V(s4, 0, W_ - 3), V(s2, 0, W_ - 3), V(s2, 2, W_ - 1), ALU.add)
            S.tt('dve', V(s8, 0, W_ - 7), V(s4, 0, W_ - 7), V(s4, 4, W_ - 3), ALU.add)
            S.tt('pool', V(s16, 0, W_ - 15), V(s8, 0, W_ - 15), V(s8, 8, W_ - 7), ALU.add)
            Om = O(m_)
            for gi4, (src, off) in enumerate(((s2, 14), (s4, 12), (s8, 8), (s16, 0))):
                c, p0 = gi4 // 2, (gi4 % 2) * 64
                if not sample:
                    dstm = m_[p0:p0 + 64, c, 0:n]
                else:
                    dstm = o4(m_)[p0:p0 + 64, c, :, :]
                S.tt('dve' if gi4 % 2 == 0 else 'pool', dstm, P(src, c, p0, off, off + nn), IV(c, p0), ALU.mult)
            S.tt('dve', O(d_), O(m_), U, ALU.subtract)
            y_ = yp.nxt()
            for c in range(2):
                ps = PS.nxt()
                S.mm(ps[:, 0:n], wblk[:, c, :], d_[:, c, 0:n])
                S.ts('dve', y_[:, c, 0:n], ps[:, 0:n], scl[:, c:c + 1], None, ALU.mult)
            S.dma('pool', ypv[:, :, t0:t0 + n], y_[:, :, 0:n])
        S.flush()

    attention(nc, S, E, l)
    rwkv(nc, S, E, l)
    mix_out(nc, S, E, l)
    ffn_ple(nc, S, E, l)


def attention(nc, S, E, l):
    T, TT = E['T'], E['TT']
    PS, PSB = E['PS'], E['PSB']
    ident, identb, masksb = E['ident'], E['identb'], E['masksb']
    qk_tok, v_tok, og, yaT = E['qk_tok'], E['v_tok'], E['og'], E['yaT']
    with ExitStack() as ph:
        sb = lambda n, s, d=F32: ph.enter_context(nc.sbuf_tensor(f"{n}_L{l}", s, d))
        qkp = Rot(nc, ph, "at_qk", [128, 2, 128], BF, 3)
        vp = Rot(nc, ph, "at_v", [128, 2, 65], BF, 4)
        qTp = Rot(nc, ph, "at_qT", [128, 128], BF, 2)
        kTp = Rot(nc, ph, "at_kT", [128, 128], BF, 4)
        pp = Rot(nc, ph, "at_p", [128, 128], BF, 4)
        pmp = Rot(nc, ph, "at_pm", [128, 128], BF, 4)
        op_ = Rot(nc, ph, "at_o", [128, 2, 65], F32, 3)
        import os
        for gi, (w, dil) in enumerate(GROUPS):
            if os.environ.get("GRP") and str(gi) not in os.environ["GRP"]:
                continue
            Ls = T // dil
            qv = qk_tok[0:T, :].rearrange("(l d) c -> d l c", d=dil)
            vv = v_tok[0:T, :, :].rearrange("(l d) h c -> d l h c", d=dil)
            ov = og[gi].rearrange("(l d) h c -> d l h c", d=dil)
            for r in range(dil):
                kT_prev = None
                v_prev = None
                for b in range(Ls // 128):
                    S.maybe_flush()
                    qk = qkp.nxt()
                    S.dma('sp', qk[:, 0, :], qv[r, b * 128:(b + 1) * 128, gi * 128:(gi + 1) * 128])
                    S.dma('sp', qk[:, 1, :], qv[r, b * 128:(b + 1) * 128, 384 + gi * 128:384 + (gi + 1) * 128])
                    v_ = vp.nxt()
                    S.dma('act', v_[:], vv[r, b * 128:(b + 1) * 128, 2 * gi:2 * gi + 2, :])
                    pt = PSB.nxt()
                    S.tr(pt[:, 0:128], qk[:, 0, :], identb[:])
                    S.tr(pt[:, 128:256], qk[:, 1, :], identb[:])
                    qT, kT = qTp.nxt(), kTp.nxt()
                    S.cp('act', qT[:], pt[:, 0:128])
                    S.cp('dve', kT[:], pt[:, 128:256])
                    po = E['PA']
                    for hh in range(2):
                        hs = slice(hh * 64, hh * 64 + 64)
                        srcs = [(kT, v_, 3)] + ([(kT_prev, v_prev, 2)] if b > 0 else [])
                        for si, (kt_, vt_, mi) in enumerate(srcs):
                            pss = PS.nxt()
                            S.mm(pss[:, 0:128], kt_[hs, :], qT[hs, :])
                            p_ = pp.nxt()
                            S.act(p_[:], pss[:, 0:128], AF.Exp, scale=0.125)
                            pm = pmp.nxt()
                            S.tt('pool' if si == 0 else 'dve', pm[:], p_[:], masksb[:, mi - 1 if False else (2 if si == 0 else 3), :], ALU.mult)
                            S.mm(po[:, hh * 65:hh * 65 + 65], pm[:], vt_[:, hh, :], start=(si == 0), stop=(si == len(srcs) - 1))
                    o_ = op_.nxt()
                    S.cp('act', o_[:], po[:, 0:130].rearrange("p (h c) -> p h c", c=65))
                    S.dma('pool', ov[r, b * 128:(b + 1) * 128, :, :], o_[:])
                    kT_prev, v_prev = kT, v_
        S.flush()
        o3 = Rot(nc, ph, "at_o3", [128, 3, 130], F32, 2)
        ysp = Rot(nc, ph, "at_ys", [128, 2, 65], F32, 2)
        rcp = Rot(nc, ph, "at_rc", [128, 2, 1], F32, 2)
        ybp = Rot(nc, ph, "at_yb", [128, 2, 64], BF, 2)
        yTp = Rot(nc, ph, "at_yT", [128, 128], BF, 2)

        def finish(ys, m, col0):
            rc = rcp.nxt()
            S.add('dve', lambda e: e.reciprocal(rc[0:m], ys[0:m, :, 64:65]), [ys.name], [rc.name])
            yb = ybp.nxt()
            S.tt('dve', yb[0:m], ys[0:m, :, 0:64], rc[0:m].broadcast_to([m, 2, 64]), ALU.mult)
            pt = PSB.nxt()
            S.tr(pt[:, 0:m], yb[0:m].rearrange("p h c -> p (h c)"), identb[0:m, 0:m])
            yT = yTp.nxt()
            S.cp('act', yT[:, 0:m], pt[:, 0:m])
            S.dma('pool', yaT[:, col0:col0 + m], yT[:, 0:m])

        for b in range(T // 128):
            S.maybe_flush()
            o_ = o3.nxt()
            for gi in range(3):
                S.dma('sp' if gi != 1 else 'act', o_[:, gi, :], og[gi][b * 128:(b + 1) * 128].rearrange("t h c -> t (h c)"))
            ys = ysp.nxt()
            ysf = ys[:].rearrange("p h c -> p (h c)")
            S.tt('dve', ysf, o_[:, 0, :], o_[:, 1, :], ALU.add)
            S.tt('dve', ysf, ysf, o_[:, 2, :], ALU.add)
            finish(ys, 128, b * 128)
        cst = Rot(nc, ph, "as_c", [128, 256], F32, 3)
        kbp = Rot(nc, ph, "as_kb", [128, 128], BF, 3)
        vbp = [sb(f"as_vb{i}", [128, 2, 65], BF) for i in range(3)]
        for t_ in vbp:
            S.ms('pool', t_[:], 1.0)
        sm = sb("as_sm", [128, 21, 8])
        smb = sb("as_smb", [128, 21, 8], BF)
        smn = sb("as_smn", [8, 3, 8])
        smnb = sb("as_smnb", [8, 3, 8], BF)
        S.dma('sp', sm[:], E['c_smask'])
        S.dma('sp', smn[:], E['c_smask_new'])
        S.cp('dve', smb[:], sm[:])
        S.cp('dve', smnb[:], smn[:])
        qs = sb("as_q", [TS, 768], BF)
        S.dma('sp', qs[:], qk_tok[T:T + TS, :])
        vs = sb("as_v", [DS, NSQ, 6, 65], BF)
        S.dma('sp', vs[:].rearrange("t s h c -> t s (h c)"), v_tok[T:T + TS].rearrange("(s t) h c -> t s (h c)", t=DS))
        qTs = sb("as_qT", [128, 6, TS], BF)
        for j in range(6):
            pt = PSB.nxt()
            S.tr(pt[:, 0:TS], qs[:, j * 128:(j + 1) * 128], identb[0:TS, 0:TS])
            S.cp('act', qTs[:, j, :], pt[:, 0:TS])
        psp = Rot(nc, ph, "as_p", [128, 8], BF, 4)
        pmsp = Rot(nc, ph, "as_pm", [128, 8], BF, 4)
        vi = 0
        accp = Rot(nc, ph, "as_acc", [8, 130], F32, 2)
        for s in range(NSQ):
            acc = accp.nxt()
            S.ms('dve', acc[:], 0.0)
            started = [False, False]
            bi = 0
            for gi, (w, dil) in enumerate(GROUPS):
                blist = [(bb, 128) for bb in range(w // 128)] + [(-1, DS)]
                for (bb, m) in blist:
                    S.maybe_flush()
                    last = (gi == 2 and bb == -1)
                    if bb >= 0:
                        c_ = cst.nxt()
                        S.dma('sp' if bb % 2 == 0 else 'act', c_[:], E['cache'][gi][l, s, bb * 128:(bb + 1) * 128, :])
                        kb = kbp.nxt()
                        S.cp('pool', kb[:], c_[:, 0:128])
                        vb_ = vbp[vi % 3]
                        vi += 1
                        S.cp('pool', vb_[:, :, 0:64], c_[:, 128:256].rearrange("p (h c) -> p h c", c=64))
                        pt = PSB.nxt()
                        S.tr(pt[:, 0:128], kb[:], identb[:])
                        kT = kbp.nxt()
                        S.cp('act', kT[:], pt[:, 0:128])
                        mk = smb[:, bi, :]
                        bi += 1
                    po = E['PA']
                    for hh in range(2):
                        hs = slice(hh * 64, hh * 64 + 64)
                        pss = PS.nxt()
                        if bb >= 0:
                            S.mm(pss[0:m, 0:8], kT[hs, :], qTs[hs, gi, s * DS:(s + 1) * DS])
                        else:
                            S.mm(pss[0:m, 0:8], qTs[hs, 3 + gi, s * DS:(s + 1) * DS], qTs[hs, gi, s * DS:(s + 1) * DS])
                        p_ = psp.nxt()
                        S.act(p_[0:m], pss[0:m, 0:8], AF.Exp, scale=0.125)
                        pm = pmsp.nxt()
                        S.tt('dve', pm[0:m], p_[0:m], mk if bb >= 0 else smnb[:, gi, :], ALU.mult)
                        rhs = vb_[:, hh, :] if bb >= 0 else vs[:, s, 2 * gi + hh, :]
                        S.mm(po[0:8, hh * 65:hh * 65 + 65], pm[0:m], rhs, start=True, stop=True)
                    S.tt('dve', acc[:], acc[:], po[0:8, 0:130], ALU.add)
            ys = ysp.nxt()
            S.cp('act', ys[0:8], acc[:].rearrange("p (h c) -> p h c", c=65))
            finish(ys, 8, T + s * DS)
        S.flush()


def rwkv(nc, S, E, l):
    T, TT = E['T'], E['TT']
    PS = E['PS']
    ident, blkones, masks = E['ident'], E['blkones'], E['masks']
    zrT, yrT, shift_o, wkv_o = E['zrT'], E['yrT'], E['shift_o'], E['wkv_o']
    NR = 256
    tiles = [(i * NR, NR) for i in range(T // NR)] + [(T, TS)]
    zrv = zrT
    yrv = yrT
    with ExitStack() as ph:
        sb = lambda n, s, d=F32: ph.enter_context(nc.sbuf_tensor(f"{n}_L{l}", s, d))
        mu = sb("rw_mu", [128, 11])
        S.dma('sp', mu[:], E['rw_mu'][l])
        v3 = sb("rw_v3", [128, 15])
        S.dma('sp', v3[:], E['rw_vec3'][l])
        w0, a0, k_k, k_a, r_k = (v3[:, i * 3:(i + 1) * 3] for i in range(5))
        lst = sb("rw_lst", [128, 2, 384])
        S.dma('sp', lst[:, 0, :], E['rw_w2a2'][l])
        S.dma('sp', lst[:, 1, :], E['rw_g2'][l])
        w2a2 = sb("rw_w2a2b", [128, 384], BF)
        g2 = sb("rw_g2b", [128, 384], BF)
        S.cp('dve', w2a2[:], lst[:, 0, :])
        S.cp('dve', g2[:], lst[:, 1, :])
        lnb = sb("rw_lnb", [128, 2, 384])
        S.dma('sp', lnb[:, 0, :], E['rw_ln'][l, 0].partition_broadcast(128))
        S.dma('sp', lnb[:, 1, :], E['rw_ln'][l, 1].partition_broadcast(128))
        mub = sb("rw_mub", [128, 11, NR])
        S.ms('dve', mub[:], 1.0)
        for j in range(11):
            S.ts('dve' if j % 2 else 'pool', mub[:, j, :], mub[:, j, :], mu[:, j:j + 1], None, ALU.mult)
        rst = sb("rw_rst", [128, 2, 512])
        S.dma('sp', rst[:], E['c_reset'])
        ident64 = ident[0:64, 0:64]
        Hs = [sb(f"rw_H{i}", [64, 6, 64]) for i in range(2)]
        zc = Rot(nc, ph, "rw_zc", [128, 11, NR + 1], F32, 1)
        zsT = sb("rw_zs", [128, 11, NR])
        dzT = sb("rw_dz", [128, 11, NR])
        lora = sb("rw_lora", [128, NR], BF)
        sgz = sb("rw_sgz", [128, NR], BF)
        sg = sb("rw_sg", [128, 3, NR])
        aa = sb("rw_a", [128, 3, NR])
        gg = sb("rw_g", [128, 3, NR])
        kk = sb("rw_kk", [128, 3, NR])
        kk2 = sb("rw_kk2", [128, 3, NR], BF)
        kp = sb("rw_kp", [128, 3, NR])
        tmpa = sb("rw_tmpa", [128, 3, NR])
        tmpb = sb("rw_tmpb", [128, 3, NR], BF)
        bon = sb("rw_bon", [128, 3, NR])
        cum = sb("rw_cum", [128, 3, NR])
        G = sb("rw_G", [128, 3, NR])
        Gi = sb("rw_Gi", [128, 3, NR])
        Gm = sb("rw_Gm", [128, 3, NR])
        at = sb("rw_at", [128, 3, NR])
        bt = sb("rw_bt", [128, 3, NR])
        kt = sb("rw_kt", [128, 3, NR])
        rt = sb("rw_rt", [128, 3, NR])
        tokp = Rot(nc, ph, "rw_tok", [128, 5, 384], F32, 2)
        Np = Rot(nc, ph, "rw_N", [128, 128], F32, 12)
        Lp = Rot(nc, ph, "rw_L", [128, 128], F32, 12)
        Ak = Rot(nc, ph, "rw_Ak", [128, 128], F32, 6)
        Arb = Rot(nc, ph, "rw_Arb", [128, 128], F32, 6)
        Ark = Rot(nc, ph, "rw_Ark", [128, 128], F32, 6)
        Xp = Rot(nc, ph, "rw_X", [128, 128], F32, 12)
        Pm = Rot(nc, ph, "rw_Pm", [64, 64], F32, 6)
        gcp = Rot(nc, ph, "rw_gc", [64, 1], F32, 3)
        Qg = Rot(nc, ph, "rw_Qg", [64, 64], F32, 6)
        YH = Rot(nc, ph, "rw_YH", [64, 128], F32, 6)
        Yt = Rot(nc, ph, "rw_Y", [128, 384], F32, 2)
        stp = Rot(nc, ph, "rw_st", [128, 6], F32, 2)
        st2 = Rot(nc, ph, "rw_st2", [128, 6], F32, 2)
        ysq = Rot(nc, ph, "rw_ysq", [128, 384], F32, 1)
        ynT = sb("rw_ynT", [128, 3, NR])
        yo = Rot(nc, ph, "rw_yo", [128, 3, NR], BF, 2)
        wst = sb("rw_wst", [64, 6, 64])
        wso = Rot(nc, ph, "rw_wso", [64, 6, 64], F32, 2)
        hcur = 0
        S.ms('dve', Hs[0][:], 0.0)

        def chunk(c0, C, lvls, Hin, Hout, y_dst_col):
            tok = tokp.nxt()
            for qi, src in enumerate((at, bt, kt, zsT, rt)):
                for j in range(3):
                    ps = PS.nxt()
                    sv = src[:, 6 + j, c0:c0 + C] if src is zsT else src[:, j, c0:c0 + C]
                    S.tr(ps[0:C, 0:128], sv, ident[:])
                    S.cp('act' if (qi + j) % 2 == 0 else 'dve', tok[0:C, qi, j * 128:(j + 1) * 128], ps[0:C, 0:128])
            Y = Yt.nxt()
            H6 = range(6)
            fs = lambda a, h: a[(h % 2) * 64:(h % 2) * 64 + 64, h // 2, c0:c0 + C]
            ts_ = lambda qi, h: tok[0:C, qi, h * 64:(h + 1) * 64]
            N, Lm, AkT, ArbT, ArkT = {}, {}, {}, {}, {}
            for h in H6:
                for store, pool_, lh, rh, mi in ((N, Np, bt, at, 0), (Lm, Lp, at, bt, 1), (AkT, Ak, kt, at, 0),
                                                 (ArbT, Arb, bt, rt, 2), (ArkT, Ark, kt, rt, 2)):
                    dst = pool_.nxt()
                    ps = PS.nxt()
                    S.mm(ps[0:C, 0:C], fs(lh, h), fs(rh, h))
                    S.tt('dve', dst[0:C, 0:C], ps[0:C, 0:C], masks[0:C, mi, 0:C], ALU.mult)
                    store[h] = dst
            X = {}
            for h in H6:
                x_ = Xp.nxt()
                S.cp('pool', x_[0:C, 0:64], ts_(0, h))
                ps = PS.nxt()
                S.mm(ps[0:C, 0:64], AkT[h][0:C, 0:C], ts_(3, h))
                S.cp('act', x_[0:C, 64:128], ps[0:C, 0:64])
                X[h] = x_
            for lv in range(lvls):
                for h in H6:
                    ps = PS.nxt()
                    S.mm(ps[0:C, 0:128], N[h][0:C, 0:C], X[h][0:C, :])
                    xn = Xp.nxt()
                    S.tt('dve', xn[0:C, :], ps[0:C, 0:128], X[h][0:C, :], ALU.add)
                    X[h] = xn
                if lv < lvls - 1:
                    for h in H6:
                        ps1, ps2 = PS.nxt(), PS.nxt()
                        S.mm(ps1[0:C, 0:C], N[h][0:C, 0:C], Lm[h][0:C, 0:C])
                        S.mm(ps2[0:C, 0:C], Lm[h][0:C, 0:C], N[h][0:C, 0:C])
                        ln_, nn_ = Lp.nxt(), Np.nxt()
                        S.cp('act', ln_[0:C, 0:C], ps1[0:C, 0:C])
                        S.cp('act' if h % 2 == 0 else 'dve', nn_[0:C, 0:C], ps2[0:C, 0:C])
                        Lm[h], N[h] = ln_, nn_
            pm, qg, yh, gCs = {}, {}, {}, {}
            for h in H6:
                j, po = h // 2, (h % 2) * 64
                gC = G[po:po + 64, j, c0 + C - 1:c0 + C]
                if po:
                    gc0 = gcp.nxt()
                    S.cp('pool', gc0[0:64, 0:1], gC)
                    gC = gc0[0:64, 0:1]
                gCs[h] = gC
                Wm, U0 = X[h][0:C, 0:64], X[h][0:C, 64:128]
                ps = PS.nxt()
                S.mm(ps[0:64, 0:64], Wm, ts_(1, h))
                p_ = Pm.nxt()
                S.tt('dve', p_[:], ps[0:64, 0:64], ident64, ALU.add)
                pm[h] = p_
                ps = PS.nxt()
                S.mm(ps[0:64, 0:64], ts_(1, h), U0, start=True, stop=False)
                S.mm(ps[0:64, 0:64], ts_(2, h), ts_(3, h), start=False, stop=True)
                q_ = Qg.nxt()
                S.ts('dve', q_[:], ps[0:64, 0:64], gC, None, ALU.mult)
                qg[h] = q_
                ps = PS.nxt()
                S.mm(ps[0:64, 0:C], Wm, ArbT[h][0:C, 0:C], start=True, stop=False)
                S.mm(ps[0:64, 0:C], ts_(4, h), ident[0:C, 0:C], start=False, stop=True)
                y_ = YH.nxt()
                S.cp('act', y_[:, 0:C], ps[0:64, 0:C])
                yh[h] = y_
            for h in H6:
                U0 = X[h][0:C, 64:128]
                ps = PS.nxt()
                S.mm(ps[0:C, 0:64], ArbT[h][0:C, 0:C], U0, start=True, stop=False)
                S.mm(ps[0:C, 0:64], ArkT[h][0:C, 0:C], ts_(3, h), start=False, stop=False)
                S.mm(ps[0:C, 0:64], yh[h][:, 0:C], Hin[:, h, :], start=False, stop=True)
                S.cp('act', Y[0:C, h * 64:(h + 1) * 64], ps[0:C, 0:64])
                ps = PS.nxt()
                S.mm(ps[0:64, 0:64], pm[h][:], Hin[:, h, :])
                S.stt('dve', Hout[:, h, :], ps[0:64, 0:64], gCs[h], qg[h][:], ALU.mult, ALU.add)
            s1, s2_ = stp.nxt(), st2.nxt()
            Y3 = Y[0:C, :].rearrange("p (h c) -> p h c", c=64)
            S.red('dve', s1[0:C, :], Y3, ALU.add)
            sq = ysq.nxt()
            S.tt('pool', sq[0:C, :], Y[0:C, :], Y[0:C, :], ALU.mult)
            S.red('dve', s2_[0:C, :], sq[0:C, :].rearrange("p (h c) -> p h c", c=64), ALU.add)
            S.ts('dve', s1[0:C, :], s1[0:C, :], 1.0 / 64, None, ALU.mult)
            S.ts('dve', s2_[0:C, :], s2_[0:C, :], 1.0 / 64, None, ALU.mult)
            m2 = stp.nxt()
            S.tt('dve', m2[0:C, :], s1[0:C, :], s1[0:C, :], ALU.mult)
            S.tt('dve', s2_[0:C, :], s2_[0:C, :], m2[0:C, :], ALU.subtract)
            S.ts('dve', s2_[0:C, :], s2_[0:C, :], 64e-5, None, ALU.add)
            S.act(s2_[0:C, :], s2_[0:C, :], AF.Ln)
            S.act(s2_[0:C, :], s2_[0:C, :], AF.Exp, scale=-0.5)
            S.tt('dve', Y3, Y3, s1[0:C, :].unsqueeze(2).broadcast_to([C, 6, 64]), ALU.subtract)
            S.tt('dve', Y3, Y3, s2_[0:C, :].unsqueeze(2).broadcast_to([C, 6, 64]), ALU.mult)
            S.tt('pool', Y[0:C, :], Y[0:C, :], lnb[0:C, 0, :], ALU.mult)
            S.tt('pool', Y[0:C, :], Y[0:C, :], lnb[0:C, 1, :], ALU.add)
            for j in range(3):
                ps = PS.nxt()
                S.tr(ps[:, 0:C], Y[0:C, j * 128:(j + 1) * 128], ident[0:C, 0:C])
                S.cp('act', ynT[:, j, y_dst_col:y_dst_col + C], ps[:, 0:C])

        for ti, (t0, n) in enumerate(tiles):
            S.maybe_flush()
            sample = (t0 == T)
            z = zc.nxt()
            if not sample:
                if ti == 0:
                    S.ms('pool', z[:, :, 0:1], 0.0)
                    S.dma('sp', z[:, :, 1:1 + n], zrv[:, :, 0:n])
                else:
                    S.dma('sp', z[:, 0:6, 0:1 + n], zrv[:, 0:6, t0 - 1:t0 + n])
                    S.dma('act', z[:, 6:11, 0:1 + n], zrv[:, 6:11, t0 - 1:t0 + n])
                prev = z[:, :, 0:n]
            else:
                S.dma('sp', z[:, :, 1:1 + n], zrv[:, :, T:T + n])
                pv = dzT[:, :, 0:n]
                pv4 = pv.rearrange("p c (s w) -> p c s w", w=DS)
                S.cp('pool', pv4[:, :, :, 1:DS], z[:, :, 1:1 + n].rearrange("p c (s w) -> p c s w", w=DS)[:, :, :, 0:DS - 1])
                for s in range(NSQ):
                    S.dma('sp', pv4[:, :, s, 0], E['st_shift'][l, s].rearrange("(c p) -> p c", p=128), slow=True)
                prev = pv
            zcur = z[:, :, 1:1 + n]
            dz = dzT[:, :, 0:n]
            zs = zsT[:, :, 0:n]
            S.tt('dve', dz, prev, zcur, ALU.subtract)
            S.tt('pool', dz, dz, mub[:, :, 0:n], ALU.mult)
            S.tt('dve', zs, dz, zcur, ALU.add)
            r_, k_, v_ = zs[:, 0:3, :], zs[:, 3:6, :], zs[:, 6:9, :]
            S.act(lora[0:64, 0:n], zs[0:64, 9, :], AF.Tanh)
            S.cp('pool', lora[64:128, 0:n], zs[64:128, 9, :])
            S.act(sgz[:, 0:n], zs[:, 10, :], AF.Sigmoid)
            for j in range(3):
                cs_ = slice(j * 128, (j + 1) * 128)
                ps = PS.nxt()
                S.mm(ps[:, 0:n], w2a2[0:64, cs_], lora[0:64, 0:n])
                S.act(sg[:, j, 0:n], ps[:, 0:n], AF.Sigmoid, bias=w0[:, j:j + 1])
                ps = PS.nxt()
                S.mm(ps[:, 0:n], w2a2[64:128, cs_], lora[64:128, 0:n])
                S.act(aa[:, j, 0:n], ps[:, 0:n], AF.Sigmoid, bias=a0[:, j:j + 1])
                ps = PS.nxt()
                S.mm(ps[:, 0:n], g2[:, cs_], sgz[:, 0:n])
                S.cp('act', gg[:, j, 0:n], ps[:, 0:n])
                S.ts('pool', kk[:, j, 0:n], k_[:, j, :], k_k[:, j:j + 1], None, ALU.mult)
                S.tt('pool', kk2[:, j, 0:n], kk[:, j, 0:n], kk[:, j, 0:n], ALU.mult)
                ps = PS.nxt()
                S.mm(ps[:, 0:n], blkones[:], kk2[:, j, 0:n])
                S.ts('dve', tmpa[:, j, 0:n], ps[:, 0:n], 1e-24, None, ALU.max)
                S.act(tmpa[:, j, 0:n], tmpa[:, j, 0:n], AF.Ln)
                S.act(tmpa[:, j, 0:n], tmpa[:, j, 0:n], AF.Exp, scale=-0.5)
                S.tt('dve', kk[:, j, 0:n], kk[:, j, 0:n], tmpa[:, j, 0:n], ALU.mult)
                S.ts('dve', tmpa[:, j, 0:n], aa[:, j, 0:n], -1.0, k_a[:, j:j + 1], ALU.add, ALU.mult)
                S.stt('dve', kp[:, j, 0:n], tmpa[:, j, 0:n], 1.0, k_[:, j, :], ALU.add, ALU.mult)
                S.stt('pool', tmpb[:, j, 0:n], r_[:, j, :], r_k[:, j:j + 1], kp[:, j, 0:n], ALU.mult, ALU.mult)
                ps = PS.nxt()
                S.mm(ps[:, 0:n], blkones[:], tmpb[:, j, 0:n])
                S.tt('dve', bon[:, j, 0:n], ps[:, 0:n], v_[:, j, :], ALU.mult)
                S.scan(cum[:, j, 0:n], rst[:, 1 if sample else 0, 0:n], sg[:, j, 0:n], 0.0, ALU.mult, ALU.add)
            S.act(G[:, :, 0:n], cum[:, :, 0:n], AF.Exp, scale=-C0)
            S.act(Gi[:, :, 0:n], cum[:, :, 0:n], AF.Exp, scale=C0)
            S.tt('pool', tmpa[:, :, 0:n], cum[:, :, 0:n], sg[:, :, 0:n], ALU.subtract)
            S.act(Gm[:, :, 0:n], tmpa[:, :, 0:n], AF.Exp, scale=-C0)
            S.tt('dve', rt[:, :, 0:n], r_, G[:, :, 0:n], ALU.mult)
            S.tt('pool', kt[:, :, 0:n], kp[:, :, 0:n], Gi[:, :, 0:n], ALU.mult)
            S.tt('dve', tmpa[:, :, 0:n], kk[:, :, 0:n], aa[:, :, 0:n], ALU.mult)
            S.tt('dve', bt[:, :, 0:n], tmpa[:, :, 0:n], Gi[:, :, 0:n], ALU.mult)
            S.stt('pool', at[:, :, 0:n], kk[:, :, 0:n], -1.0, Gm[:, :, 0:n], ALU.mult, ALU.mult)
            if not sample:
                for c in range(n // 128):
                    S.maybe_flush()
                    chunk(c * 128, 128, 7, Hs[hcur], Hs[1 - hcur], c * 128)
                    hcur = 1 - hcur
                if t0 + n == T:
                    wo = wso.nxt()
                    for h in range(6):
                        ps = PS.nxt()
                        S.tr(ps[0:64, 0:64], Hs[hcur][:, h, :], ident[0:64, 0:64])
                        S.cp('act', wo[:, h, :], ps[0:64, 0:64])
                    S.dma('pool', wkv_o[l, 0].rearrange("h v k -> v h k"), wo[:])
            else:
                for s in range(NSQ):
                    S.dma('sp', wst[:], E['st_wkv'][l, s].rearrange("h v k -> v h k"))
                    Hi, Ho = Hs[0], Hs[1]
                    for h in range(6):
                        ps = PS.nxt()
                        S.tr(ps[0:64, 0:64], wst[:, h, :], ident[0:64, 0:64])
                        S.cp('act', Hi[:, h, :], ps[0:64, 0:64])
                    chunk(s * DS, DS, 3, Hi, Ho, s * DS)
                    wo = wso.nxt()
                    for h in range(6):
                        ps = PS.nxt()
                        S.tr(ps[0:64, 0:64], Ho[:, h, :], ident[0:64, 0:64])
                        S.cp('act', wo[:, h, :], ps[0:64, 0:64])
                    S.dma('pool', wkv_o[l, 1 + s].rearrange("h v k -> v h k"), wo[:])
            y_ = yo.nxt()
            S.tt('dve', ynT[:, :, 0:n], ynT[:, :, 0:n], bon[:, :, 0:n], ALU.add)
            S.tt('pool', y_[:, :, 0:n], ynT[:, :, 0:n], gg[:, :, 0:n], ALU.mult)
            S.dma('pool', yrv[:, :, t0:t0 + n], y_[:, :, 0:n])
        S.flush()


def mix_out(nc, S, E, l):
    T, TT = E['T'], E['TT']
    PS = E['PS']
    xT, gT, ypT, yaT, yrT = E['xT'], E['gT'], E['ypT'], E['yaT'], E['yrT']
    tiles = [(i * 512, 512) for i in range(T // 512)] + [(T, TS)]
    xTv = xT
    gTv = gT
    with ExitStack() as ph:
        sb = lambda n, s, d=F32: ph.enter_context(nc.sbuf_tensor(f"{n}_L{l}", s, d))
        stage = Rot(nc, ph, "mo_st", [128, D], F32, 3)
        wproj = sb("mo_wp", [128, 6, D], BF)
        wo = sb("mo_wo", [128, KC, D], BF)
        load_cast(S, stage, wproj[:, 0:2, :], E['proj_pool'][l], 2)
        load_cast(S, stage, wproj[:, 2:3, :], E['proj_attn'][l], 1)
        load_cast(S, stage, wproj[:, 3:6, :], E['proj_rwkv'][l], 3)
        load_cast(S, stage, wo, E['w_out'][l], KC)
        yp = Rot(nc, ph, "mo_y", [128, 6, 512], BF, 2)
        gp = Rot(nc, ph, "mo_g", [128, 24, 512], BF, 2)
        xp = Rot(nc, ph, "mo_x", [128, KC, 512], F32, 2)
        mg = Rot(nc, ph, "mo_mg", [128, KC, 512], BF, 2)
        m1p = Rot(nc, ph, "mo_m1", [128, 512], F32, 2)
        m2p = Rot(nc, ph, "mo_m2", [128, 512], F32, 2)
        for (t0, n) in tiles:
            S.maybe_flush()
            y_ = yp.nxt()
            S.dma('sp', y_[:, 0:2, 0:n], ypT[:, :, t0:t0 + n])
            S.dma('sp', y_[:, 2, 0:n], yaT[:, t0:t0 + n])
            S.dma('sp', y_[:, 3:6, 0:n], yrT[:, :, t0:t0 + n])
            g_ = gp.nxt()
            S.dma('act', g_[:, :, 0:n], gTv[:, :, t0:t0 + n])
            x = xp.nxt()
            S.dma('sp', x[:, :, 0:n], xTv[:, :, t0:t0 + n])
            m_ = mg.nxt()
            for oc in range(KC):
                cs_ = slice(oc * 128, (oc + 1) * 128)
                ps = PS.nxt()
                for c in range(2):
                    S.mm(ps[:, 0:n], wproj[:, c, cs_], y_[:, c, 0:n], start=(c == 0), stop=(c == 1))
                m1 = m1p.nxt()
                S.tt('dve', m1[:, 0:n], ps[:, 0:n], g_[:, oc, 0:n], ALU.mult)
                ps = PS.nxt()
                S.mm(ps[:, 0:n], wproj[:, 2, cs_], y_[:, 2, 0:n])
                m2 = m2p.nxt()
                S.tt('dve', m2[:, 0:n], ps[:, 0:n], g_[:, 8 + oc, 0:n], ALU.mult)
                S.tt('pool', m1[:, 0:n], m1[:, 0:n], m2[:, 0:n], ALU.add)
                ps = PS.nxt()
                for c in range(3):
                    S.mm(ps[:, 0:n], wproj[:, 3 + c, cs_], y_[:, 3 + c, 0:n], start=(c == 0), stop=(c == 2))
                m2 = m2p.nxt()
                S.tt('dve', m2[:, 0:n], ps[:, 0:n], g_[:, 16 + oc, 0:n], ALU.mult)
                S.tt('pool', m_[:, oc, 0:n], m1[:, 0:n], m2[:, 0:n], ALU.add)
            for oc in range(KC):
                cs_ = slice(oc * 128, (oc + 1) * 128)
                ps = PS.nxt()
                for kc in range(KC):
                    S.mm(ps[:, 0:n], wo[:, kc, cs_], m_[:, kc, 0:n], start=(kc == 0), stop=(kc == KC - 1))
                S.tt('dve', x[:, oc, 0:n], x[:, oc, 0:n], ps[:, 0:n], ALU.add)
            S.dma('pool', xTv[:, :, t0:t0 + n], x[:, :, 0:n])
        S.flush()


def ffn_ple(nc, S, E, l):
    T, TT, L = E['T'], E['TT'], E['L']
    PS = E['PS']
    ident, onesb = E['ident'], E['onesb']
    xT = E['xT']
    NT = 512
    tiles = [(i * NT, NT) for i in range(T // NT)] + [(T, TS)]
    xTv = xT
    with ExitStack() as ph:
        sb = lambda n, s, d=F32: ph.enter_context(nc.sbuf_tensor(f"{n}_L{l}", s, d))
        stage = Rot(nc, ph, "ff_st", [128, FF], F32, 1)
        w1 = sb("ff_w1", [128, KC, FF], BF)
        w3 = sb("ff_w3", [128, KC, FF], BF)
        w2 = sb("ff_w2", [128, FC, D], BF)
        load_cast(S, stage, w1, E['ffn_w1'][l], KC)
        load_cast(S, stage, w3, E['ffn_w3'][l], KC)
        load_cast(S, stage, w2, E['ffn_w2'][l], FC)
        gv = sb("ff_g", [128, 3 * KC])
        S.dma('sp', gv[:], E['vec8'][l])
        xp = Rot(nc, ph, "ff_x", [128, KC, NT], F32, 1)
        hp = Rot(nc, ph, "ff_h", [128, KC, NT], BF, 1)
        sqp = Rot(nc, ph, "ff_sq", [128, KC, NT], BF, 1)
        rsp = Rot(nc, ph, "ff_rs", [128, NT], F32, 1)
        hid = Rot(nc, ph, "ff_hid", [128, FC, NT], BF, 1)
        sil = Rot(nc, ph, "ff_sil", [128, NT], F32, 2)
        for (t0, n) in tiles:
            S.maybe_flush()
            x = xp.nxt()
            S.dma('sp', x[:, :, 0:n], xTv[:, :, t0:t0 + n])
            h = hp.nxt()
            rmsnorm_tile(S, PS, x, h, gv[:, KC:2 * KC], sqp, rsp, onesb, n)
            hd = hid.nxt()
            for fc in range(FC):
                cs_ = slice(fc * 128, (fc + 1) * 128)
                pa, pb_ = PS.nxt(), PS.nxt()
                for kc in range(KC):
                    S.mm(pa[:, 0:n], w1[:, kc, cs_], h[:, kc, 0:n], start=(kc == 0), stop=(kc == KC - 1))
                for kc in range(KC):
                    S.mm(pb_[:, 0:n], w3[:, kc, cs_], h[:, kc, 0:n], start=(kc == 0), stop=(kc == KC - 1))
                sl_ = sil.nxt()
                S.act(sl_[:, 0:n], pa[:, 0:n], AF.Silu)
                S.tt('dve', hd[:, fc, 0:n], sl_[:, 0:n], pb_[:, 0:n], ALU.mult)
            for oc in range(KC):
                cs_ = slice(oc * 128, (oc + 1) * 128)
                ps = PS.nxt()
                for fc in range(FC):
                    S.mm(ps[:, 0:n], w2[:, fc, cs_], hd[:, fc, 0:n], start=(fc == 0), stop=(fc == FC - 1))
                S.tt('dve', x[:, oc, 0:n], x[:, oc, 0:n], ps[:, 0:n], ALU.add)
            S.dma('pool', xTv[:, :, t0:t0 + n], x[:, :, 0:n])
        S.flush()
    NT = 512
    tiles = [(i * NT, NT) for i in range(T // NT)] + [(T, TS)]
    with ExitStack() as ph:
        sb = lambda n, s, d=F32: ph.enter_context(nc.sbuf_tensor(f"{n}_L{l}", s, d))
        stage = Rot(nc, ph, "pl_st", [128, D], F32, 3)
        wg = sb("pl_wg", [128, KC, D], BF)
        wp = sb("pl_wp", [128, 2, D], BF)
        load_cast(S, stage, wg, E['ple_gate'][l], KC)
        load_cast(S, stage, wp, E['ple_proj'][l], 2)
        gv = sb("pl_g", [128, 3 * KC])
        S.dma('sp', gv[:], E['vec8'][l])
        xp = Rot(nc, ph, "pl_x", [128, KC, NT], F32, 2)
        hp = Rot(nc, ph, "pl_h", [128, KC, NT], BF, 2)
        sqp = Rot(nc, ph, "pl_sq", [128, KC, NT], BF, 1)
        rsp = Rot(nc, ph, "pl_rs", [128, NT], F32, 2)
        pin = Rot(nc, ph, "pl_pin", [128, 4, PLE], F32, 2)
        pT = Rot(nc, ph, "pl_pT", [128, 2, NT], BF, 2)
        gt = Rot(nc, ph, "pl_gt", [128, NT], F32, 3)
        for (t0, n) in tiles:
            S.maybe_flush()
            x = xp.nxt()
            S.dma('sp', x[:, :, 0:n], xTv[:, :, t0:t0 + n])
            nb = (n + 127) // 128
            pb = min(n, 128)
            pi = pin.nxt()
            S.dma('act', pi[0:pb, 0:nb, :], E['p_in'][l, t0:t0 + n, :].rearrange("(b p) c -> p b c", p=pb))
            pt_ = pT.nxt()
            for b in range(nb):
                ps = PS.nxt()
                for c in range(2):
                    S.tr(ps[:, c * 128:c * 128 + pb], pi[0:pb, b, c * 128:(c + 1) * 128], ident[0:pb, 0:pb])
                S.cp('act', pt_[:, :, b * 128:b * 128 + pb], ps[:, 0:256].rearrange("p (c t) -> p c t", c=2)[:, :, 0:pb])
            h2 = hp.nxt()
            rmsnorm_tile(S, PS, x, h2, gv[:, 2 * KC:3 * KC], sqp, rsp, onesb, n)
            for oc in range(KC):
                cs_ = slice(oc * 128, (oc + 1) * 128)
                ps = PS.nxt()
                for kc in range(KC):
                    S.mm(ps[:, 0:n], wg[:, kc, cs_], h2[:, kc, 0:n], start=(kc == 0), stop=(kc == KC - 1))
                g_ = gt.nxt()
                S.act(g_[:, 0:n], ps[:, 0:n], AF.Sigmoid)
                ps = PS.nxt()
                for c in range(2):
                    S.mm(ps[:, 0:n], wp[:, c, cs_], pt_[:, c, 0:n], start=(c == 0), stop=(c == 1))
                S.tt('dve', g_[:, 0:n], g_[:, 0:n], ps[:, 0:n], ALU.mult)
                S.tt('pool', x[:, oc, 0:n], x[:, oc, 0:n], g_[:, 0:n], ALU.add)
            S.dma('pool', xTv[:, :, t0:t0 + n], x[:, :, 0:n])
        S.flush()


def final_out(nc, S, E):
    T = E['T']
    PS = E['PS']
    ident, onesb = E['ident'], E['onesb']
    xT, y_out = E['xT'], E['y_out']
    NT = 512
    tiles = [(i * NT, NT) for i in range(T // NT)] + [(T, TS)]
    xTv = xT
    with ExitStack() as ph:
        gf = ph.enter_context(nc.sbuf_tensor("fo_gf", [128, KC], F32))
        S.dma('sp', gf[:], E['vecf'])
        xp = Rot(nc, ph, "fo_x", [128, KC, NT], F32, 2)
        hp = Rot(nc, ph, "fo_h", [128, KC, NT], F32, 2)
        sqp = Rot(nc, ph, "fo_sq", [128, KC, NT], BF, 1)
        rsp = Rot(nc, ph, "fo_rs", [128, NT], F32, 2)
        yt = Rot(nc, ph, "fo_yt", [128, D], F32, 2)
        for (t0, n) in tiles:
            S.maybe_flush()
            x = xp.nxt()
            S.dma('sp', x[:, :, 0:n], xTv[:, :, t0:t0 + n])
            y_ = hp.nxt()
            rmsnorm_tile(S, PS, x, y_, gf, sqp, rsp, onesb, n)
            nb = (n + 127) // 128
            pb = min(n, 128)
            for b in range(nb):
                yt_ = yt.nxt()
                for half in range(2):
                    ps = PS.nxt()
                    for q in range(4):
                        kc = half * 4 + q
                        S.tr(ps[0:pb, q * 128:(q + 1) * 128], y_[:, kc, b * 128:b * 128 + pb], ident[:])
                    S.cp('act' if half == 0 else 'dve', yt_[0:pb, half * 512:(half + 1) * 512], ps[0:pb, :])
                S.dma('pool', y_out[t0 + b * 128:t0 + b * 128 + pb, :], yt_[0:pb, :])
        S.flush()


def host_consts(T):
    TT = T + TS
    half = 32
    inv = (10000.0 ** (-2.0 * np.arange(half, dtype=np.float32) / 64)).astype(np.float32)
    pos = np.concatenate([np.arange(T, dtype=np.float32)] + [8192.0 + np.arange(DS, dtype=np.float32)] * NSQ)
    ang = pos[:, None].astype(np.float32) * inv[None, :]
    cos = np.tile(np.cos(ang).astype(np.float32), (1, 12))
    sin = np.tile(np.sin(ang).astype(np.float32), (1, 12))
    ident = np.eye(128, dtype=np.float32)
    blk = np.zeros((128, 128), np.float32)
    blk[:64, :64] = 1
    blk[64:, 64:] = 1
    r = np.arange(128)[:, None]
    c = np.arange(128)[None, :]
    masks = np.stack([(c > r), (c < r), (c >= r), (c <= r)], axis=1).astype(np.float32)
    invcnt = np.zeros((128, 2, 2, 512), np.float32)
    for g4, w in enumerate((2, 4, 8, 16)):
        cidx, p0 = g4 // 2, (g4 % 2) * 64
        t = np.arange(512)
        invcnt[p0:p0 + 64, 0, cidx, :] = 1.0 / np.minimum(t + 1, w)
        invcnt[p0:p0 + 64, 1, cidx, :] = 1.0 / w
    smask = np.zeros((128, 21, 8), np.float32)
    smask_new = np.zeros((8, 3, 8), np.float32)
    bi = 0
    for gi, (w, dil) in enumerate(GROUPS):
        for bb in range(w // 128):
            i = bb * 128 + np.arange(128)[:, None]
            t = np.arange(8)[None, :]
            dist = w + t - i
            smask[:, bi, :] = ((dist >= 0) & (dist <= 128 * dil) & (dist % dil == 0)).astype(np.float32)
            bi += 1
        j = np.arange(8)[:, None]
        t = np.arange(8)[None, :]
        dist = t - j
        smask_new[:, gi, :] = ((dist >= 0) & (dist % dil == 0)).astype(np.float32)
    reset = np.ones((128, 2, 512), np.float32)
    reset[:, 0, ::128] = 0
    reset[:, 1, ::8] = 0
    return dict(c_cos=cos, c_sin=sin, c_ident=ident, c_blk=blk, c_masks=masks, c_invcnt=invcnt,
                c_smask=smask, c_smask_new=smask_new, c_reset=reset)


def pcol(v, n):
    v = np.asarray(v, np.float32)
    return np.ascontiguousarray(np.swapaxes(v.reshape(v.shape[:-1] + (n, 128)), -1, -2))


def make_in_maps(inp, T, L, n_cores=8, l0=0, xT_prev=None):
    PER_LAYER = {'state_pool', 'state_shift', 'state_wkv', 'cache_kv_w128', 'cache_kv_w512', 'cache_kv_w2048', 'p_prompt',
                 'p_sample', 'norm_mix_g', 'w_in', 'pool_w_grp', 'pool_scale', 'rwkv_mu', 'rwkv_w0', 'rwkv_w2', 'rwkv_a0',
                 'rwkv_a2', 'rwkv_g2', 'rwkv_k_k', 'rwkv_k_a', 'rwkv_r_k', 'rwkv_ln_g', 'rwkv_ln_b', 'proj_pool', 'proj_attn',
                 'proj_rwkv', 'w_out', 'norm_ffn_g', 'ffn_w1', 'ffn_w3', 'ffn_w2', 'norm_ple_g', 'ple_proj', 'ple_gate'}
    f = lambda k: (np.asarray(inp[k], np.float32)[l0:l0 + L] if k in PER_LAYER else np.asarray(inp[k], np.float32))
    consts = host_consts(T)
    wblk = np.zeros((L, 2, 128, 128), np.float32)
    pw = f('pool_w_grp')
    for g4 in range(4):
        c, p0 = g4 // 2, (g4 % 2) * 64
        wblk[:, c, p0:p0 + 64, p0:p0 + 64] = pw[:, g4]
    vec8 = np.concatenate([pcol(f('norm_mix_g'), 8), pcol(f('norm_ffn_g'), 8), pcol(f('norm_ple_g'), 8)], axis=-1)
    vec3 = np.concatenate([pcol(f('rwkv_w0'), 3), pcol(f('rwkv_a0'), 3), pcol(f('rwkv_k_k'), 3), pcol(f('rwkv_k_a'), 3),
                           pcol(f('rwkv_r_k').reshape(L, 384), 3)], axis=-1)
    shared = dict(
        w_in=f('w_in'), pool_wblk=wblk, vec8=np.ascontiguousarray(vec8), vecf=pcol(f('norm_final_g'), 8),
        pool_scale=pcol(f('pool_scale'), 2), rw_mu=pcol(f('rwkv_mu'), 11), rw_vec3=np.ascontiguousarray(vec3),
        rw_w2a2=np.ascontiguousarray(np.concatenate([f('rwkv_w2'), f('rwkv_a2')], axis=1)), rw_g2=f('rwkv_g2'),
        rw_ln=np.ascontiguousarray(np.stack([f('rwkv_ln_g'), f('rwkv_ln_b')], axis=1)),
        proj_pool=f('proj_pool'), proj_attn=f('proj_attn'), proj_rwkv=f('proj_rwkv'), w_out=f('w_out'),
        ffn_w1=f('ffn_w1'), ffn_w3=f('ffn_w3'), ffn_w2=f('ffn_w2'), ple_proj=f('ple_proj'), ple_gate=f('ple_gate'),
        **consts)
    maps = []
    xp, xs, pp, ps_ = f('x_prompt'), f('x_sample'), f('p_prompt'), f('p_sample')
    for c in range(n_cores):
        b = c % xp.shape[0]
        sq = slice(c * NSQ, (c + 1) * NSQ)
        m = dict(shared)
        if xT_prev is None:
            m['x_in'] = np.ascontiguousarray(np.concatenate([xp[b, :T], xs[sq].reshape(TS, D)], axis=0))
        else:
            m['xT_in'] = np.ascontiguousarray(xT_prev[c])
        m['p_in'] = np.ascontiguousarray(np.concatenate([pp[:L, b, :T], ps_[:L, sq].reshape(L, TS, PLE)], axis=1))
        m['st_pool'] = np.ascontiguousarray(f('state_pool')[:L, sq])
        m['st_shift'] = np.ascontiguousarray(f('state_shift')[:L, sq])
        m['st_wkv'] = np.ascontiguousarray(f('state_wkv')[:L, sq])
        for (w, _), k in zip(GROUPS, ('cache_kv_w128', 'cache_kv_w512', 'cache_kv_w2048')):
            m[f'cache{w}'] = np.ascontiguousarray(f(k)[:L, sq].reshape(L, NSQ, w, 256))
        maps.append(m)
    return maps


_NC_CACHE = {}


def run(inp, T, L, debug=False, first=True, last=True, l0=0, xT_prev=None):
    key = (T, L, debug, first, last)
    if key not in _NC_CACHE:
        _NC_CACHE[key] = build(T, L, debug, first, last)
    nc = _NC_CACHE[key]
    maps = make_in_maps(inp, T, L, l0=l0, xT_prev=xT_prev)
    res = run_bass_kernel_spmd(nc, maps, core_ids=list(range(8)))
    return res.results


def assemble(R, T, L, NB):
    y_p = np.stack([R[b]['y_out'][:T] for b in range(NB)])
    y_s = np.concatenate([R[c]['y_out'][T:].reshape(NSQ, DS, D) for c in range(8)])
    def ps(name, tail):
        p = np.stack([R[b][name][:, 0] for b in range(NB)], axis=1)
        s = np.concatenate([R[c][name][:, 1:] for c in range(8)], axis=1)
        return p.reshape((L, NB) + tail), s.reshape((L, 8 * NSQ) + tail)
    pool_p, pool_s = ps('pool_o', (15, 256))
    sh_p, sh_s = ps('shift_o', (RW,))
    wk_p, wk_s = ps('wkv_o', (6, 64, 64))
    outs = [y_p, y_s, pool_p, pool_s, sh_p, sh_s, wk_p, wk_s]
    for w, _ in GROUPS:
        a, b = ps(f'kv{w}_o', (w, 2, 2, 64))
        outs += [a, b]
    return tuple(np.ascontiguousarray(o, dtype=np.float32) for o in outs)


def kernel(**inputs):
    T, L = 8192, 4
    R = run(inputs, T, L)
    return assemble(R, T, L, 2)
```
